# Optimizing a Trainium2 kernel written in Bass

```python
import jax, jax.numpy as jnp
from jax import lax
import numpy as np

D_MODEL = 2048
BATCH = 4
SEQ = 2048
DEPTH = 1
DEC_BATCH = 128
DEC_SEQ = 8
PAST_LEN = 16384
PAGE_SIZE = 128

N_META = 16
G_RWKV = D_MODEL // 2
HEAD_DIM = 64
N_HEADS = G_RWKV // HEAD_DIM
G_CONV = D_MODEL - G_RWKV
CONV_WIDTH = 3
W_LORA = max(32, int(round(1.8 * D_MODEL ** 0.5 / 32)) * 32)
A_LORA = max(32, int(round(1.8 * D_MODEL ** 0.5 / 32)) * 32)
G_LORA = max(32, int(round(0.6 * D_MODEL ** 0.8 / 32)) * 32)
RWKV_PROJ = 3 * G_RWKV + W_LORA + A_LORA + G_LORA
P_TOTAL = RWKV_PROJ + 3 * G_CONV
RWKV_SPLITS = [int(s) for s in np.cumsum([G_RWKV, W_LORA, G_RWKV, G_RWKV, A_LORA])]
D_FF = 5632
RMS_EPS = 1e-6
GN_EPS = 64e-5
F32 = jnp.float32

kernel_name = "hymba_rwkv7_shortconv_macaron_step"

LAYER_KEYS = ("g_ffn1", "ffn1_gate", "ffn1_up", "ffn1_down", "g_mix", "w_in", "mu_shift",
              "w0", "w_lora_w", "a0", "w_lora_a", "w_lora_g", "k_k", "k_a", "r_k",
              "ln_x_w", "ln_x_b", "conv_w", "w_out", "g_ffn2", "ffn2_gate", "ffn2_up", "ffn2_down")


def _rms_norm(x, g):
    xf = x.astype(F32)
    y = xf * lax.rsqrt(jnp.mean(xf * xf, axis=-1, keepdims=True) + RMS_EPS)
    return (y * g.astype(F32)).astype(x.dtype)


def _swiglu(h, w_gate, w_up, w_down):
    return (jax.nn.silu(h @ w_gate) * (h @ w_up)) @ w_down


def _wkv7_scan(S0, r, decay, k, v, kk, a):
    xs = tuple(jnp.moveaxis(t, 1, 0) for t in (r, decay, k, v, kk, a))

    def step(S, inp):
        r_t, w_t, k_t, v_t, kk_t, a_t = inp
        sa = jnp.einsum("bhij,bhj->bhi", S, -kk_t)
        S = (S * w_t[:, :, None, :] + sa[..., None] * (kk_t * a_t)[:, :, None, :]
             + v_t[..., None] * k_t[:, :, None, :])
        return S, jnp.einsum("bhij,bhj->bhi", S, r_t)

    S_T, y = lax.scan(step, S0, xs)
    return S_T, jnp.moveaxis(y, 0, 1)


def _mixer(h, wkv0, prev0, buf0, p):
    B, T, _ = h.shape
    proj = h @ p["w_in"]
    p_rwkv, p_conv = proj[..., :RWKV_PROJ], proj[..., RWKV_PROJ:]

    shifted = jnp.concatenate([prev0[:, None, :].astype(proj.dtype), p_rwkv[:, :-1]], axis=1)
    q = (p_rwkv + (shifted - p_rwkv) * p["mu_shift"]).astype(F32)
    new_prev = p_rwkv[:, -1]
    r, wd, k, v, ad, gd = jnp.split(q, RWKV_SPLITS, axis=-1)
    w_log = -jax.nn.softplus(-(p["w0"].astype(F32) + jnp.tanh(wd) @ p["w_lora_w"].astype(F32))) - 0.5
    decay = jnp.exp(-jnp.exp(w_log))
    a = jax.nn.sigmoid(p["a0"].astype(F32) + ad @ p["w_lora_a"].astype(F32))
    g = jax.nn.sigmoid(gd) @ p["w_lora_g"].astype(F32)
    heads = lambda t: t.reshape(B, T, N_HEADS, HEAD_DIM)
    kk = heads(k * p["k_k"].astype(F32))
    kk = kk / jnp.maximum(jnp.sqrt(jnp.sum(kk * kk, axis=-1, keepdims=True)), 1e-12)
    k = k * (1.0 + (a - 1.0) * p["k_a"].astype(F32))
    r_h, k_h, v_h = heads(r), heads(k), heads(v)
    S_T, y = _wkv7_scan(wkv0.astype(F32), r_h, heads(decay), k_h, v_h, kk, heads(a))
    mu = jnp.mean(y, axis=-1, keepdims=True)
    var = jnp.mean(jnp.square(y - mu), axis=-1, keepdims=True)
    y_n = ((y - mu) * lax.rsqrt(var + GN_EPS)).reshape(B, T, G_RWKV)
    y_n = y_n * p["ln_x_w"].astype(F32) + p["ln_x_b"].astype(F32)
    bonus = (jnp.sum(r_h * k_h * p["r_k"].astype(F32), axis=-1, keepdims=True) * v_h).reshape(B, T, G_RWKV)
    rwkv_out = ((y_n + bonus) * g).astype(h.dtype)

    b_gate, c_gate, x_in = jnp.split(p_conv, 3, axis=-1)
    u = c_gate * x_in
    full = jnp.concatenate([buf0.astype(u.dtype), u], axis=1)
    cw = p["conv_w"]
    conv = sum(cw[j] * full[:, j:j + T] for j in range(CONV_WIDTH))
    conv_out = (b_gate * conv).astype(h.dtype)
    new_buf = full[:, -(CONV_WIDTH - 1):]

    mix = jnp.concatenate([rwkv_out, conv_out], axis=-1) @ p["w_out"]
    return mix, S_T, new_prev, new_buf


def _layer(x, wkv0, prev0, buf0, p):
    x = x + 0.5 * _swiglu(_rms_norm(x, p["g_ffn1"]), p["ffn1_gate"], p["ffn1_up"], p["ffn1_down"])
    mix, wkv1, prev1, buf1 = _mixer(_rms_norm(x, p["g_mix"]), wkv0, prev0, buf0, p)
    x = x + mix
    x = x + 0.5 * _swiglu(_rms_norm(x, p["g_ffn2"]), p["ffn2_gate"], p["ffn2_up"], p["ffn2_down"])
    return x, wkv1, prev1, buf1


def _trunk(x, wkv, prev, buf, layers, g_final):
    new_wkv, new_prev, new_buf = [], [], []
    for i in range(DEPTH):
        x, s, pr, bf = _layer(x, wkv[i], prev[i], buf[i], layers[i])
        new_wkv.append(s); new_prev.append(pr); new_buf.append(bf)
    return _rms_norm(x, g_final), jnp.stack(new_wkv), jnp.stack(new_prev), jnp.stack(new_buf)


def setup_inputs(seed: int = 0) -> dict:
    key = jax.random.key(seed)
    ks = iter(jax.random.split(key, 40))
    nrm = lambda shape, s: jax.random.normal(next(ks), shape, F32) * s
    L = DEPTH
    return {
        "x_prompt": nrm((BATCH, SEQ, D_MODEL), 1.0),
        "x_sample": nrm((DEC_BATCH, DEC_SEQ, D_MODEL), 1.0),
        "state_wkv": nrm((L, DEC_BATCH, N_HEADS, HEAD_DIM, HEAD_DIM), 0.5),
        "state_shift": nrm((L, DEC_BATCH, RWKV_PROJ), 1.0),
        "state_conv": nrm((L, DEC_BATCH, CONV_WIDTH - 1, G_CONV), 1.0),
        "meta_tokens": nrm((N_META, D_MODEL), 1.0),
        "g_ffn1": 1.0 + nrm((L, D_MODEL), 0.05),
        "ffn1_gate": nrm((L, D_MODEL, D_FF), D_MODEL ** -0.5),
        "ffn1_up": nrm((L, D_MODEL, D_FF), D_MODEL ** -0.5),
        "ffn1_down": nrm((L, D_FF, D_MODEL), D_FF ** -0.5),
        "g_mix": 1.0 + nrm((L, D_MODEL), 0.05),
        "w_in": nrm((L, D_MODEL, P_TOTAL), D_MODEL ** -0.5),
        "mu_shift": jax.random.uniform(next(ks), (L, RWKV_PROJ), F32),
        "w0": -2.0 + nrm((L, G_RWKV), 0.5),
        "w_lora_w": nrm((L, W_LORA, G_RWKV), 0.5 * W_LORA ** -0.5),
        "a0": nrm((L, G_RWKV), 0.1),
        "w_lora_a": nrm((L, A_LORA, G_RWKV), 0.5 * A_LORA ** -0.5),
        "w_lora_g": nrm((L, G_LORA, G_RWKV), G_LORA ** -0.5),
        "k_k": 0.85 + nrm((L, G_RWKV), 0.05),
        "k_a": 1.0 + nrm((L, G_RWKV), 0.05),
        "r_k": nrm((L, N_HEADS, HEAD_DIM), 0.1),
        "ln_x_w": 1.0 + nrm((L, G_RWKV), 0.05),
        "ln_x_b": nrm((L, G_RWKV), 0.01),
        "conv_w": nrm((L, CONV_WIDTH, G_CONV), CONV_WIDTH ** -0.5),
        "w_out": nrm((L, D_MODEL, D_MODEL), D_MODEL ** -0.5),
        "g_ffn2": 1.0 + nrm((L, D_MODEL), 0.05),
        "ffn2_gate": nrm((L, D_MODEL, D_FF), D_MODEL ** -0.5),
        "ffn2_up": nrm((L, D_MODEL, D_FF), D_MODEL ** -0.5),
        "ffn2_down": nrm((L, D_FF, D_MODEL), D_FF ** -0.5),
        "g_final": 1.0 + nrm((D_MODEL,), 0.05),
    }


def reference(x_prompt, x_sample, state_wkv, state_shift, state_conv, meta_tokens,
              g_ffn1, ffn1_gate, ffn1_up, ffn1_down, g_mix, w_in, mu_shift, w0, w_lora_w,
              a0, w_lora_a, w_lora_g, k_k, k_a, r_k, ln_x_w, ln_x_b, conv_w, w_out,
              g_ffn2, ffn2_gate, ffn2_up, ffn2_down, g_final):
    layers = [
        {"g_ffn1": g_ffn1[i], "ffn1_gate": ffn1_gate[i], "ffn1_up": ffn1_up[i],
         "ffn1_down": ffn1_down[i], "g_mix": g_mix[i], "w_in": w_in[i], "mu_shift": mu_shift[i],
         "w0": w0[i], "w_lora_w": w_lora_w[i], "a0": a0[i], "w_lora_a": w_lora_a[i],
         "w_lora_g": w_lora_g[i], "k_k": k_k[i], "k_a": k_a[i], "r_k": r_k[i],
         "ln_x_w": ln_x_w[i], "ln_x_b": ln_x_b[i], "conv_w": conv_w[i], "w_out": w_out[i],
         "g_ffn2": g_ffn2[i], "ffn2_gate": ffn2_gate[i], "ffn2_up": ffn2_up[i],
         "ffn2_down": ffn2_down[i]}
        for i in range(DEPTH)]

    B = x_prompt.shape[0]
    meta = jnp.broadcast_to(meta_tokens.astype(x_prompt.dtype)[None], (B, N_META, D_MODEL))
    xp = jnp.concatenate([meta, x_prompt], axis=1)
    wkv_p0 = jnp.zeros((DEPTH, B, N_HEADS, HEAD_DIM, HEAD_DIM), F32)
    prev_p0 = jnp.zeros((DEPTH, B, RWKV_PROJ), F32)
    buf_p0 = jnp.zeros((DEPTH, B, CONV_WIDTH - 1, G_CONV), F32)
    yp, wkv_p, prev_p, buf_p = _trunk(xp, wkv_p0, prev_p0, buf_p0, layers, g_final)
    y_prompt = yp[:, N_META:]

    y_sample, wkv_s, prev_s, buf_s = _trunk(x_sample, state_wkv, state_shift, state_conv, layers, g_final)

    return (y_prompt, y_sample,
            wkv_p.astype(state_wkv.dtype), prev_p.astype(state_shift.dtype), buf_p.astype(state_conv.dtype),
            wkv_s.astype(state_wkv.dtype), prev_s.astype(state_shift.dtype), buf_s.astype(state_conv.dtype))
```

```python
import numpy as np
from contextlib import ExitStack
import concourse.bass as bass
import concourse.mybir as mybir
from concourse.bass_utils import run_bass_kernel_spmd

F32 = mybir.dt.float32
BF16 = mybir.dt.bfloat16
AF = mybir.ActivationFunctionType
ALU = mybir.AluOpType

D = 2048
DFF = 5632
NF = DFF // 128
KC = D // 128
NH = 16
HD = 64
GR = 1024
RWKV_PROJ = 3520
P_TOTAL = 6592
N_META = 16
RMS_EPS = 1e-6
GN_EPS = 64e-5
NT = 3
N = NT * 128
NTILES = 18
NBLK = NTILES // NT
POST_BLK0 = 3
NPOST = (NBLK - POST_BLK0) * NT
SAMPLE_TILE = NTILES - 1
LAST_PROMPT_TILE = NTILES - 2
NSQ = 16
LSQ = 8
OFF_R, OFF_WD, OFF_K, OFF_V, OFF_AD, OFF_GD = 0, 1024, 1120, 2144, 3168, 3264
OFF_CB, OFF_CC, OFF_CX = 3520, 4544, 5568
WSL = 4
DSL = 3
PREFETCH_W = 2
PREFETCH_D = 2
FSPLIT = 2
NFH = NF // FSPLIT


class Prog:
    def __init__(self, nc, es):
        self.nc = nc
        self.dry = False
        self.engs = {"pe": nc.tensor, "dve": nc.vector, "act": nc.scalar, "pool": nc.gpsimd, "sp": nc.sync}
        self.sem = {k: es.enter_context(nc.semaphore("s_" + k)) for k in self.engs}
        self.cnt = {k: 0 for k in self.engs}
        self.waited = {k: {} for k in self.engs}
        self.last_w = {}
        self.readers = {}
        self.NS = 8
        self.dsem = {q: [es.enter_context(nc.semaphore(f"d_{q}{i}")) for i in range(self.NS)] for q in ("sp", "pool")}
        self.dcnt = {q: 0 for q in ("sp", "pool")}

    def _wait(self, eng, sem, val):
        if eng == "pe" and sem is self.sem["pe"]:
            return
        key = id(sem)
        if self.waited[eng].get(key, 0) < val:
            self.engs[eng].wait_ge(sem, val)
            self.waited[eng][key] = val

    def _deps(self, eng, reads, writes):
        deps = []
        for r in reads:
            lw = self.last_w.get(r)
            if lw:
                deps.append(lw)
        for w in writes:
            lw = self.last_w.get(w)
            if lw:
                deps.append(lw)
            deps.extend(self.readers.get(w, {}).values())
        for sem, val in deps:
            self._wait(eng, sem, val)

    def _commit(self, done, reads, writes):
        for r in reads:
            d = self.readers.setdefault(r, {})
            old = d.get(id(done[0]))
            if old is None or old[1] < done[1]:
                d[id(done[0])] = done
        for w in writes:
            self.last_w[w] = done
            self.readers[w] = {}

    def op(self, eng, fn, reads=(), writes=()):
        if self.dry:
            return
        ps = [r for r in reads if r.startswith("ps")]
        if ps:
            reads = [r for r in reads if not r.startswith("ps")]
            writes = list(writes) + ps
        self._deps(eng, reads, writes)
        inst = fn(self.engs[eng])
        self.cnt[eng] += 1
        inst.then_inc(self.sem[eng], 1)
        self._commit((self.sem[eng], self.cnt[eng]), reads, writes)

    def dma(self, q, out, in_, reads=(), writes=()):
        if self.dry:
            return
        self._deps(q, reads, writes)
        i = self.dcnt[q]
        self.dcnt[q] += 1
        sem = self.dsem[q][i % self.NS]
        val = 16 * (i // self.NS + 1)
        if val > 16:
            self._wait(q, sem, val - 16)
        self.engs[q].dma_start(out=out, in_=in_).then_inc(sem, 16)
        self._commit((sem, val), reads, writes)

    def barrier(self):
        if self.dry:
            return
        for eng in self.engs:
            for k in self.engs:
                if self.cnt[k] > 0:
                    self._wait(eng, self.sem[k], self.cnt[k])
            for q in ("sp", "pool"):
                for k in range(self.NS):
                    n = (self.dcnt[q] - k + self.NS - 1) // self.NS
                    if n > 0:
                        self._wait(eng, self.dsem[q][k], 16 * n)

    def finish(self):
        for q in ("sp", "pool"):
            for k in range(self.NS):
                n = (self.dcnt[q] - k + self.NS - 1) // self.NS
                if n > 0:
                    self._wait("sp", self.dsem[q][k], 16 * n)
        for k in self.engs:
            if k != "sp" and self.cnt[k] > 0:
                self._wait("sp", self.sem[k], self.cnt[k])


def psr(b, c0, c1):
    return [f"ps{b}"]


def build_program(cfg=None):
    cfg = cfg or {}
    nc = bass.Bass("TRN2", target_bir_lowering=False)

    def din(name, shape):
        return nc.dram_tensor(name, list(shape), F32, kind="ExternalInput").ap()

    def dout(name, shape):
        return nc.dram_tensor(name, list(shape), F32, kind="ExternalOutput").ap()

    xw = din("xw", [NTILES * 128, D])
    s_wkv = din("s_wkv", [NSQ, NH, HD, HD])
    s_shift = din("s_shift", [NSQ, RWKV_PROJ])
    s_conv = din("s_conv", [NSQ, 2, GR])
    w1g = din("w1g", [D, DFF]); w1u = din("w1u", [D, DFF]); w1d = din("w1d", [DFF, D])
    w2g = din("w2g", [D, DFF]); w2u = din("w2u", [D, DFF]); w2d = din("w2d", [DFF, D])
    w_in = din("w_in", [D, P_TOTAL]); w_out = din("w_out", [D, D])
    lw_w = din("lw_w", [96, GR]); lw_a = din("lw_a", [96, GR]); lw_g = din("lw_g", [256, GR])
    gvec = din("gvec", [128, 3 * KC])
    gfin = din("gfin", [128, D])
    mu_fm = din("mu_fm", [128, 28])
    pv = din("pv", [128, 8, 8])
    cw = din("cw", [128, 8, 3])
    cmat = din("cmat", [128, 2048])
    cmask_d = din("cmask", [128, NSQ * 128])
    rmask_d = din("rmask", [128, NSQ])

    WMETA = {"w1g": w1g, "w1u": w1u, "w1d": w1d, "w2g": w2g, "w2u": w2u, "w2d": w2d, "w_in": w_in, "w_out": w_out}
    WB16 = {k: nc.dram_tensor("b16_" + k, list(v.shape), BF16, kind="Internal").ap() for k, v in WMETA.items()}
    CROWS = 256
    wdone = {}

    y_out = dout("y_out", [NPOST * 128, D])
    wkv_p = dout("wkv_p", [NH, HD, HD])
    shift_p = dout("shift_p", [RWKV_PROJ])
    conv_p = dout("conv_p", [2, GR])
    wkv_s = dout("wkv_s", [NSQ, NH, HD, HD])
    shift_s = dout("shift_s", [NSQ, RWKV_PROJ])
    conv_s = dout("conv_s", [NSQ, 2, GR])

    with ExitStack() as es:
        es.enter_context(nc.allow_non_contiguous_dma(reason="small strided state rows"))

        def T(name, shape, dt=F32):
            return es.enter_context(nc.sbuf_tensor("sb_" + name, list(shape), dt))

        PS = [es.enter_context(nc.psum_tensor(f"psb{i}", [128, 512], F32)) for i in range(8)]
        pg = Prog(nc, es)

        xb = [T(f"xb{i}", [128, D]) for i in range(NT)]
        xq = [T(f"xq{i}", [128, 512]) for i in range(2)]
        hT = T("hT", [128, KC, N], BF16)
        act = T("act", [128, NFH, N], BF16)
        wsl = [T(f"wsl{i}", [128, KC, 256], BF16) for i in range(WSL)]
        dsl = [T(f"dsl{i}", [128, 1024], BF16) for i in range(DSL)]
        stat = T("stat", [128, 8])
        junk = act[:, 0:6, :].rearrange("p a b -> p (a b)")[:, 0:D]
        JUNK = [f"act{i}" for i in range(6)]
        cm = T("cm", [128, 2048])
        cmask = T("cmask", [128, NSQ, 128])
        rmask = T("rmask", [128, NSQ])
        gv = T("gv", [128, 3 * KC])
        gf = T("gf", [128, D])
        mu = T("mu", [128, 28])
        omu = T("omu", [128, 28])
        pvt = T("pvt", [128, 8, 8])
        cwt = T("cwt", [128, 8, 3])
        lww = T("lww", [96, GR], BF16)
        lwa = T("lwa", [96, GR], BF16)
        lwg = T("lwg", [128, 2, GR], BF16)
        ident = cm[:, 0:128]
        BO = cm[:, 128:256]
        BO64 = cm[:, 256:384]
        maskA = {1: cm[:, 384:640], NSQ: cm[:, 768:1024]}
        maskL = {1: cm[:, 640:768], NSQ: cm[:, 1024:1152]}
        scanm = {1: cm[:, 1152:1280], NSQ: cm[:, 1280:1408]}
        def TB(name, extra=0, dt=F32):
            return T(name, [128, N + extra], dt)
        pb = [TB(f"pb{i}", 1) for i in range(3)]
        qr, qk, qv = TB("qr"), TB("qk"), TB("qv")
        lgw, av, gg, kk, kp, ka, bon, t1, t2 = (TB(n) for n in ("lgw", "av", "gg", "kk", "kp", "ka", "bon", "t1", "t2"))
        gg2, bon2 = TB("gg2"), TB("bon2")
        GB = [(gg, "gg", bon, "bon"), (gg2, "gg2", bon2, "bon2")]
        twd = T("twd", [96, N], BF16); qad = T("qad", [96, N], BF16); sgd = T("sgd", [128, 2, N], BF16)
        carry = T("carry", [128, 28])
        prevS = T("prevS", [128, NSQ, LSQ])
        shT = T("shT", [128, 28, NSQ])
        shO = T("shO", [128, 28, NSQ])
        shP = T("shP", [128, 28])
        ub = T("ub", [128, N + 2])
        ucar = T("ucar", [128, 8, 2])
        uS = T("uS", [128, NSQ, LSQ + 2])
        cvt = t2
        arena = T("arena", [128, 8960])
        _ao = [0]

        def AV(words, dt=F32, shape3=None):
            a = _ao[0]
            _ao[0] += words
            assert _ao[0] <= 8960
            v = arena[:, a:a + words]
            if dt == BF16:
                v = v.bitcast(BF16)
            return v

        Lam, Lc, Le, G1, G2, G3, G4 = (AV(128) for _ in range(7))
        AR = AV(256).rearrange("p (a t) -> p a t", t=128)
        bt, kt, bh, kh = (AV(128) for _ in range(4))
        Vtm, Btm, Ktm = (AV(128) for _ in range(3))
        MB = [AV(256) for h in range(2)]
        MK = [AV(256) for h in range(2)]
        PQ = [AV(512).rearrange("p (a t) -> p a t", t=128) for i in range(2)]
        XX = [AV(128) for i in range(2)]
        Ysb, Yc, Ysq, Yr = (AV(128) for _ in range(4))
        E1 = AV(2048).rearrange("p (n t) -> p n t", t=128)
        E2 = AV(2048).rearrange("p (n t) -> p n t", t=128)
        _ao[0] = 0
        nLam, nLc, nLe, nG1, nG2, nG3, nG4, nbh, nkh = (AV(N) for _ in range(9))
        nAR = AV(NT * 128, BF16).rearrange("p (n a t) -> p n a t", a=2, t=128)
        nbt = AV(N // 2, BF16)
        nkt = AV(N // 2, BF16)
        nVBK = AV(NT * 3 * 64, BF16).rearrange("p (n a t) -> p n a t", a=3, t=128)
        nMB = AV(NT * 256, BF16).rearrange("p (n a t) -> p n a t", a=2, t=256)
        nMK = AV(NT * 256, BF16).rearrange("p (n a t) -> p n a t", a=2, t=256)
        nPQ = [AV(NT * 256, BF16).rearrange("p (n a t) -> p n a t", a=4, t=128) for i in range(2)]
        nXb = AV(NT * 128, BF16).rearrange("p (n a t) -> p n a t", a=2, t=128)
        nRH = AV(N // 2, BF16)
        nMS = AV(NT * 32, BF16).rearrange("p (n t) -> p n t", t=64)
        stg = [arena[0:32, 4096 + i * 128:4096 + (i + 1) * 128] for i in range(4)]
        ctmp = arena[:, 4608:4640]
        Sb = T("Sb", [128, 8, HD], BF16)
        identb = T("identb", [128, 128], BF16)
        I2 = T("I2", [128, 64])
        scanP = T("scanP", [128, N])
        Sp = T("Sp", [128, 8, HD])
        Ss = T("Ss", [128, NSQ, HD])
        Sld = E1[0:64]
        Sst = E2[0:64]

        wplan = {"w": [], "d": []}
        wissued = {"w": 0, "d": 0}
        wnext = {"w": 0, "d": 0}

        def issue_w(pool, idx):
            spec = wplan[pool][idx]
            name = spec[0]
            conv = wdone.get(name, False)
            src = WB16[name] if conv else WMETA[name]
            q = "sp" if conv else "pool"
            if pool == "w":
                s = idx % WSL
                _, c0, width = spec
                rds = [f"scr_{name}_{k}" for k in range(src.shape[0] // CROWS)] if conv else []
                pg.dma(q, wsl[s][:, :, 0:width],
                       src.rearrange("(kc p) f -> p kc f", p=128)[:, :, c0:c0 + width],
                       reads=rds, writes=[f"wsl{s}"])
            else:
                s = idx % DSL
                _, r0, c0 = spec
                rds = [f"scr_{name}_{r0 // CROWS}"] if conv else []
                pg.dma(q, dsl[s][:], src[r0:r0 + 128, c0:c0 + 1024], reads=rds, writes=[f"dsl{s}"])

        conv_plan = {0: [], 1: [], 2: []}
        if cfg.get("convert", True):
            for blk_, names in ((0, ("w1g", "w1u")), (1, ("w1d", "w2g")), (2, ("w2u", "w2d"))):
                for nm in names:
                    for k in range(WMETA[nm].shape[0] // CROWS):
                        conv_plan[blk_].append((nm, k))
        conv_pos = {}

        def convert_slot(blk, c):
            items = conv_plan.get(blk, [])
            pos = conv_pos.get(blk, 0)
            n = -(-(len(items) - pos) // (8 - c))
            for nm, k in items[pos:pos + n]:
                pg.dma("pool", WB16[nm][k * CROWS:(k + 1) * CROWS, :], WMETA[nm][k * CROWS:(k + 1) * CROWS, :],
                       writes=[f"scr_{nm}_{k}"])
            conv_pos[blk] = pos + n
            if pos + n >= len(items):
                for nm, _ in items:
                    wdone[nm] = True

        def wreq(pool, spec):
            idx = wnext[pool]
            wnext[pool] += 1
            if pg.dry:
                wplan[pool].append(spec)
            else:
                depth = PREFETCH_W if pool == "w" else PREFETCH_D
                while wissued[pool] < min(len(wplan[pool]), idx + 1 + depth):
                    issue_w(pool, wissued[pool])
                    wissued[pool] += 1
            if pool == "w":
                s = idx % WSL
                return wsl[s], f"wsl{s}"
            s = idx % DSL
            return dsl[s], f"dsl{s}"

        def proj_from(slot, res, sc0, width, bank, n=N):
            for kc in range(KC):
                pg.op("pe", lambda e, kc=kc: e.matmul(PS[bank][0:width, 0:n], lhsT=slot[:, kc, sc0:sc0 + width],
                                                        rhs=hT[:, kc, 0:n], start=(kc == 0), stop=(kc == KC - 1)),
                      reads=[res, "hT"], writes=psr(bank, 0, n))

        def proj(src, c0, width, bank, n=N):
            slot, res = wreq("w", (src, c0, width))
            proj_from(slot, res, 0, width, bank, n)

        def norm_T(gcol):
            for tt in range(NT):
                xr = f"xb{tt}"
                pg.op("act", lambda e, tt=tt: e.activation(out=junk[:], in_=xb[tt][:], func=AF.Square,
                                                            accum_out=stat[:, 0:1]),
                      reads=[xr], writes=JUNK + ["stat0"])
                pg.op("dve", lambda e: e.tensor_scalar(out=stat[:, 1:2], in0=stat[:, 0:1], scalar1=1.0 / D,
                                                       scalar2=RMS_EPS, op0=ALU.mult, op1=ALU.add),
                      reads=["stat0"], writes=["stat1"])
                pg.op("act", lambda e: e.activation(out=stat[:, 1:2], in_=stat[:, 1:2], func=AF.Sqrt),
                      reads=["stat1"], writes=["stat1"])
                pg.op("dve", lambda e: e.reciprocal(out=stat[:, 2:3], in_=stat[:, 1:2]),
                      reads=["stat1"], writes=["stat2"])
                for g4 in range(4):
                    if cfg.get("nt_stage", 3) < 2:
                        break
                    bank = g4 % 2
                    xq_ = xq[g4 % 2]
                    xqr = f"xq{g4 % 2}"
                    pg.op("dve", lambda e, tt=tt, g4=g4, xq_=xq_: e.tensor_scalar(out=xq_[:], in0=xb[tt][:, g4 * 512:(g4 + 1) * 512],
                                                                                scalar1=stat[:, 2:3], scalar2=None, op0=ALU.mult),
                          reads=[xr, "stat2"], writes=[xqr])
                    for j in range(4):
                        pg.op("pe", lambda e, j=j, bank=bank, xq_=xq_: e.transpose(
                            out=PS[bank][:, j * 128:(j + 1) * 128], in_=xq_[:, j * 128:(j + 1) * 128], identity=ident),
                            reads=[xqr, "cm"], writes=psr(bank, j * 128, (j + 1) * 128))
                    if cfg.get("nt_stage", 3) < 3:
                        continue
                    pg.op("dve", lambda e, g4=g4, bank=bank, tt=tt: e.tensor_tensor(
                        out=hT[:, g4 * 4:(g4 + 1) * 4, tt * 128:(tt + 1) * 128],
                        in0=PS[bank][:, 0:512].rearrange("p (c t) -> p c t", t=128),
                        in1=gv[:, gcol + g4 * 4:gcol + (g4 + 1) * 4].unsqueeze(2).to_broadcast([128, 4, 128]),
                        op=ALU.mult),
                        reads=psr(bank, 0, 512) + ["gv"], writes=["hT"])

        def down_proj(nchunk, src, row0, scale, tiles):
            for half in range(2):
                for c in range(nchunk):
                    slot, res = wreq("d", (src, row0 + c * 128, half * 1024))
                    for ti, tt in enumerate(tiles):
                        for bb in range(2):
                            bank = 2 + ti * 2 + bb
                            pg.op("pe", lambda e, c=c, tt=tt, bb=bb, bank=bank: e.matmul(
                                PS[bank][:, 0:512], lhsT=act[:, c, tt * 128:(tt + 1) * 128],
                                rhs=slot[:, bb * 512:(bb + 1) * 512], start=(c == 0), stop=(c == nchunk - 1)),
                                reads=[res, f"act{c}"], writes=psr(bank, 0, 512))
                for ti, tt in enumerate(tiles):
                    for bb in range(2):
                        bank = 2 + ti * 2 + bb
                        col = half * 1024 + bb * 512
                        pg.op("dve", lambda e, tt=tt, bank=bank, col=col: e.scalar_tensor_tensor(
                            out=xb[tt][:, col:col + 512], in0=PS[bank][:, 0:512], scalar=scale,
                            in1=xb[tt][:, col:col + 512], op0=ALU.mult, op1=ALU.add),
                            reads=psr(bank, 0, 512) + [f"xb{tt}"], writes=[f"xb{tt}"])

        def ffn(wg, wu, wd):
            for fh in range(FSPLIT):
                for fl in range(NFH):
                    f = fh * NFH + fl
                    gb = (fl % 2) * 2
                    if fl % 2 == 0:
                        gslot, gres = wreq("w", (wg, f * 128, 256))
                        uslot, ures = wreq("w", (wu, f * 128, 256))
                    proj_from(gslot, gres, (fl % 2) * 128, 128, gb)
                    proj_from(uslot, ures, (fl % 2) * 128, 128, gb + 1)
                    pg.op("act", lambda e, gb=gb: e.activation(out=t1[:], in_=PS[gb][:, 0:N], func=AF.Silu),
                          reads=psr(gb, 0, N), writes=["t1"])
                    pg.op("dve", lambda e, gb=gb, fl=fl: e.tensor_tensor(out=act[:, fl, :], in0=t1[:],
                                                                         in1=PS[gb + 1][:, 0:N], op=ALU.mult),
                          reads=["t1"] + psr(gb + 1, 0, N), writes=[f"act{fl}"])
                down_proj(NFH, wd, fh * NFH * 128, 0.5, list(range(NT)))

        def dv(fn, reads, writes, eng="dve"):
            pg.op(eng, fn, reads=reads, writes=writes)

        def chunk_id(kind, c=0):
            return {"r": c, "wd": 8, "k": 9 + c, "v": 17 + c, "ad": 25, "gd": 26 + c}[kind]

        CH_OFF = {}
        for c in range(8):
            CH_OFF[chunk_id("r", c)] = (OFF_R + c * 128, 128)
            CH_OFF[chunk_id("k", c)] = (OFF_K + c * 128, 128)
            CH_OFF[chunk_id("v", c)] = (OFF_V + c * 128, 128)
        CH_OFF[8] = (OFF_WD, 96)
        CH_OFF[25] = (OFF_AD, 96)
        CH_OFF[26] = (OFF_GD, 128)
        CH_OFF[27] = (OFF_GD + 128, 128)

        def lerp_chunk(cid, bank, pbuf, pres, q, qres, has_sample, last_prompt, npc):
            off, w = CH_OFF[cid]
            dv(lambda e: e.tensor_copy(out=pbuf[0:w, 0:1], in_=carry[0:w, cid:cid + 1]), ["carry"], [pres])
            dv(lambda e: e.activation(out=pbuf[0:w, 1:N + 1], in_=PS[bank][0:w, 0:N], func=AF.Copy),
               psr(bank, 0, N), [pres], eng="act")
            if npc > 0:
                dv(lambda e: e.tensor_copy(out=carry[0:w, cid:cid + 1], in_=pbuf[0:w, npc:npc + 1]), [pres], ["carry"])
            dv(lambda e: e.tensor_scalar(out=q[0:w, 0:N], in0=pbuf[0:w, 1:N + 1], scalar1=omu[0:w, cid:cid + 1],
                                         scalar2=None, op0=ALU.mult), [pres, "omu"], [qres])
            if npc > 0:
                dv(lambda e: e.scalar_tensor_tensor(out=q[0:w, 0:npc], in0=pbuf[0:w, 0:npc], scalar=mu[0:w, cid:cid + 1],
                                                    in1=q[0:w, 0:npc], op0=ALU.mult, op1=ALU.add),
                   [pres, "mu", qres], [qres])
            if last_prompt:
                dv(lambda e: e.tensor_copy(out=shP[0:w, cid:cid + 1], in_=pbuf[0:w, npc:npc + 1]), [pres], ["shP"],
                   eng="pool")
            if has_sample:
                s0 = npc + 1
                pS = pbuf[0:w, s0:s0 + 128].rearrange("p (n t) -> p n t", t=LSQ)
                dv(lambda e: e.tensor_copy(out=prevS[0:w, :, 1:LSQ], in_=pS[:, :, 0:LSQ - 1]), [pres], ["prevS"])
                dv(lambda e: e.tensor_copy(out=prevS[0:w, :, 0:1], in_=shT[0:w, cid, :].unsqueeze(2)), ["shT"], ["prevS"])
                dv(lambda e: e.tensor_copy(out=shO[0:w, cid, :].unsqueeze(2), in_=pS[:, :, LSQ - 1:LSQ]), [pres], ["shO"],
                   eng="pool")
                qS = q[0:w, npc:npc + 128].rearrange("p (n t) -> p n t", t=LSQ)
                dv(lambda e: e.scalar_tensor_tensor(out=qS, in0=prevS[0:w, :, :], scalar=mu[0:w, cid:cid + 1],
                                                    in1=qS, op0=ALU.mult, op1=ALU.add),
                   ["prevS", "mu", qres], [qres])

        def scan_tile(c, col0, nseq, is_last_prompt, gb=None):
            gg_, ggr, bon_, bonr = gb if gb else GB[0]
            L = 128 // nseq
            cs = slice(col0, col0 + 128)
            S = Sp[:, c:c + 1, :] if nseq == 1 else Ss[:, :, :]
            Sres = f"Sp{c}" if nseq == 1 else "Ss"
            nlev = 7 if nseq == 1 else 3
            dv(lambda e: e.tensor_tensor_scan(out=Lam[:], data0=scanm[nseq], data1=lgw[:, cs], initial=0.0,
                                              op0=ALU.mult, op1=ALU.add), ["cm", "lgw"], ["Lam"])
            Lv = Lam[:].rearrange("p (n t) -> p n t", t=L)
            dv(lambda e: e.tensor_tensor(out=Lc[:].rearrange("p (n t) -> p n t", t=L),
                                         in0=Lv[:, :, L - 1:L].to_broadcast([128, nseq, L]), in1=Lv, op=ALU.subtract),
               ["Lam"], ["Lc"])
            dv(lambda e: e.tensor_tensor(out=Le[:], in0=Lam[:], in1=lgw[:, cs], op=ALU.subtract), ["Lam", "lgw"], ["Le"],
               eng="pool")
            dv(lambda e: e.activation(out=G1[:], in_=Lam[:], func=AF.Exp), ["Lam"], ["G1"], eng="act")
            dv(lambda e: e.activation(out=G2[:], in_=Lam[:], func=AF.Exp, scale=-1.0), ["Lam"], ["G2"], eng="act")
            dv(lambda e: e.activation(out=G3[:], in_=Le[:], func=AF.Exp), ["Le"], ["G3"], eng="act")
            dv(lambda e: e.activation(out=G4[:], in_=Lc[:], func=AF.Exp), ["Lc"], ["G4"], eng="act")
            dv(lambda e: e.scalar_tensor_tensor(out=AR[:, 0, :], in0=kk[:, cs], scalar=-1.0, in1=G3[:],
                                                op0=ALU.mult, op1=ALU.mult), ["kk", "G3"], ["AR0"])
            dv(lambda e: e.tensor_tensor(out=AR[:, 1, :], in0=qr[:, cs], in1=G1[:], op=ALU.mult), ["qr", "G1"], ["AR1"])
            dv(lambda e: e.tensor_tensor(out=bt[:], in0=ka[:, cs], in1=G2[:], op=ALU.mult), ["ka", "G2"], ["bt"])
            dv(lambda e: e.tensor_tensor(out=kt[:], in0=kp[:, cs], in1=G2[:], op=ALU.mult), ["kp", "G2"], ["kt"])
            dv(lambda e: e.tensor_tensor(out=bh[:], in0=ka[:, cs], in1=G4[:], op=ALU.mult), ["ka", "G4"], ["bh"], eng=cfg.get("bh_eng", "pool"))
            dv(lambda e: e.tensor_tensor(out=kh[:], in0=kp[:, cs], in1=G4[:], op=ALU.mult), ["kp", "G4"], ["kh"], eng=cfg.get("bh_eng", "pool"))
            if cfg.get('sc_stage', 9) <= 1:
                return
            for j, (src, sres) in enumerate(((qv[:, cs], "qv"), (bh[:], "bh"), (kh[:], "kh"))):
                if j >= cfg.get("ntr", 3):
                    break
                pg.op("pe", lambda e, j=j, src=src: e.transpose(out=PS[0][:, j * 128:(j + 1) * 128], in_=src, identity=ident),
                      reads=[sres, "cm"], writes=psr(0, j * 128, (j + 1) * 128))
            dv(lambda e: e.activation(out=Vtm[:], in_=PS[0][:, 0:128], func=AF.Copy), psr(0, 0, 128), ["Vtm"], eng="act")
            if nseq == 1 and cfg.get("ntr", 3) == 3:
                dv(lambda e: e.tensor_copy(out=Btm[:], in_=PS[0][:, 128:256]), psr(0, 128, 256), ["Btm"])
                dv(lambda e: e.activation(out=Ktm[:], in_=PS[0][:, 256:384], func=AF.Copy), psr(0, 256, 384), ["Ktm"], eng="act")
            if cfg.get('sc_sub', 9) <= 1:
                return
            for h in range(2):
                hs = slice(h * 64, (h + 1) * 64)
                pg.op("pe", lambda e, hs=hs, h=h: e.matmul(PS[1][:, h * 256:(h + 1) * 256], lhsT=bt[hs, :],
                                                            rhs=AR[hs, :, :].rearrange("p a t -> p (a t)"), start=True, stop=True),
                      reads=["bt", "AR0", "AR1"], writes=psr(1, h * 256, (h + 1) * 256))
                pg.op("pe", lambda e, hs=hs, h=h: e.matmul(PS[2][:, h * 256:(h + 1) * 256], lhsT=kt[hs, :],
                                                            rhs=AR[hs, :, :].rearrange("p a t -> p (a t)"), start=True, stop=True),
                      reads=["kt", "AR0", "AR1"], writes=psr(2, h * 256, (h + 1) * 256))
                pg.op("pe", lambda e, hs=hs, h=h: e.matmul(PS[3][:, h * 128:(h + 1) * 128], lhsT=AR[hs, 0, :],
                                                            rhs=bt[hs, :], start=True, stop=True),
                      reads=["bt", "AR0"], writes=psr(3, h * 128, (h + 1) * 128))
            if cfg.get('sc_sub', 9) <= 2:
                return
            for h in range(2):
                dv(lambda e, h=h: e.tensor_tensor(out=MB[h][:], in0=PS[1][:, h * 256:(h + 1) * 256], in1=maskA[nseq], op=ALU.mult),
                   psr(1, h * 256, (h + 1) * 256) + ["cm"], [f"MB{h}"])
                dv(lambda e, h=h: e.tensor_tensor(out=MK[h][:], in0=PS[2][:, h * 256:(h + 1) * 256], in1=maskA[nseq], op=ALU.mult),
                   psr(2, h * 256, (h + 1) * 256) + ["cm"], [f"MK{h}"])
                dv(lambda e, h=h: e.tensor_tensor(out=PQ[0][:, h, :], in0=PS[3][:, h * 128:(h + 1) * 128], in1=maskL[nseq], op=ALU.mult),
                   psr(3, h * 128, (h + 1) * 128) + ["cm"], ["PQ0"])
                dv(lambda e, h=h: e.tensor_copy(out=PQ[0][:, 2 + h, :], in_=MB[h][:, 0:128]), [f"MB{h}"], ["PQ0"], eng="pool")
            if cfg.get('sc_stage', 9) <= 2:
                return
            if nseq > 1:
                dv(lambda e: e.tensor_tensor(out=E1[:], in0=AR[:, 0:1, :].to_broadcast([128, nseq, 128]), in1=cmask[:], op=ALU.mult),
                   ["AR0", "cmask"], ["E1"])
                dv(lambda e: e.tensor_tensor(out=E2[:], in0=AR[:, 1:2, :].to_broadcast([128, nseq, 128]), in1=cmask[:], op=ALU.mult),
                   ["AR1", "cmask"], ["E2"], eng="pool")
            for h in range(2):
                hs = slice(h * 64, (h + 1) * 64)
                for n in range(nseq):
                    lhs = AR[hs, 0, :] if nseq == 1 else E1[hs, n, :]
                    pg.op("pe", lambda e, hs=hs, h=h, n=n, lhs=lhs: e.matmul(PS[4][:, h * 64:(h + 1) * 64], lhsT=lhs,
                                                                                rhs=S[hs, n, :], start=(n == 0), stop=False),
                          reads=["AR0", "E1", Sres], writes=psr(4, 0, 128))
                pg.op("pe", lambda e, h=h: e.matmul(PS[4][:, h * 64:(h + 1) * 64], lhsT=MK[h][:, 0:128],
                                                     rhs=Vtm[:, h * 64:(h + 1) * 64], start=False, stop=True),
                      reads=[f"MK{h}", "Vtm"], writes=psr(4, 0, 128))
            dv(lambda e: e.tensor_copy(out=XX[0][:], in_=PS[4][:, 0:128]), psr(4, 0, 128), ["XX0"])
            if cfg.get('sc_stage', 9) <= 3:
                return
            cur = 0
            for lev in range(nlev):
                pq = PQ[lev % 2]
                pqr = f"PQ{lev % 2}"
                for h in range(2):
                    pg.op("pe", lambda e, h=h, pq=pq, cur=cur: e.matmul(PS[4][:, 128 + h * 64:128 + (h + 1) * 64], lhsT=pq[:, 2 + h, :],
                                                                          rhs=XX[cur][:, h * 64:(h + 1) * 64], start=True, stop=True),
                          reads=[pqr, f"XX{cur}"], writes=psr(4, 128, 256))
                dv(lambda e, cur=cur: e.tensor_tensor(out=XX[1 - cur][:], in0=XX[cur][:], in1=PS[4][:, 128:256], op=ALU.add),
                   [f"XX{cur}"] + psr(4, 128, 256), [f"XX{1 - cur}"])
                cur = 1 - cur
                if lev < nlev - 1:
                    nq = PQ[(lev + 1) % 2]
                    nqr = f"PQ{(lev + 1) % 2}"
                    for h in range(2):
                        pg.op("pe", lambda e, h=h, pq=pq: e.matmul(PS[5][:, h * 128:(h + 1) * 128], lhsT=pq[:, 2 + h, :],
                                                                    rhs=pq[:, h, :], start=True, stop=True),
                              reads=[pqr], writes=psr(5, h * 128, (h + 1) * 128))
                        pg.op("pe", lambda e, h=h, pq=pq: e.matmul(PS[5][:, (2 + h) * 128:(3 + h) * 128], lhsT=pq[:, h, :],
                                                                    rhs=pq[:, 2 + h, :], start=True, stop=True),
                              reads=[pqr], writes=psr(5, (2 + h) * 128, (3 + h) * 128))
                    dv(lambda e, nq=nq: e.activation(out=nq[:].rearrange("p a t -> p (a t)"), in_=PS[5][:, 0:512], func=AF.Copy),
                       psr(5, 0, 512), [nqr], eng="act")
            U = XX[cur]
            Ures = f"XX{cur}"
            if cfg.get('sc_stage', 9) <= 4:
                return
            for h in range(2):
                hs = slice(h * 64, (h + 1) * 64)
                for n in range(nseq):
                    rhs = AR[hs, 1, :] if nseq == 1 else E2[hs, n, :]
                    pg.op("pe", lambda e, hs=hs, n=n, rhs=rhs: e.matmul(PS[6][hs, 0:128], lhsT=S[hs, n, :], rhs=rhs,
                                                                          start=(n == 0), stop=False),
                          reads=["AR1", "E2", Sres], writes=psr(6, 0, 128))
                pg.op("pe", lambda e, hs=hs, h=h: e.matmul(PS[6][hs, 0:128], lhsT=U[:, h * 64:(h + 1) * 64], rhs=MB[h][:, 128:256],
                                                            start=False, stop=False),
                      reads=[Ures, f"MB{h}"], writes=psr(6, 0, 128))
                pg.op("pe", lambda e, hs=hs, h=h: e.matmul(PS[6][hs, 0:128], lhsT=Vtm[:, h * 64:(h + 1) * 64], rhs=MK[h][:, 128:256],
                                                            start=False, stop=True),
                      reads=["Vtm", f"MK{h}"], writes=psr(6, 0, 128))
            dv(lambda e: e.activation(out=Ysb[:], in_=PS[6][:, 0:128], func=AF.Copy), psr(6, 0, 128), ["Ysb"], eng="act")
            if cfg.get('sc_stage', 9) <= 5:
                return
            if nseq > 1:
                dv(lambda e: e.tensor_tensor(out=E1[:], in0=PS[0][:, 128:256].unsqueeze(1).to_broadcast([128, nseq, 128]),
                                             in1=rmask[:].unsqueeze(2).to_broadcast([128, nseq, 128]), op=ALU.mult),
                   psr(0, 128, 256) + ["rmask"], ["E1"])
                dv(lambda e: e.tensor_tensor(out=E2[:], in0=PS[0][:, 256:384].unsqueeze(1).to_broadcast([128, nseq, 128]),
                                             in1=rmask[:].unsqueeze(2).to_broadcast([128, nseq, 128]), op=ALU.mult),
                   psr(0, 256, 384) + ["rmask"], ["E2"])
            for n in range(nseq):
                sl = n % 4
                lb = Btm[:] if nseq == 1 else E1[:, n, :]
                lk = Ktm[:] if nseq == 1 else E2[:, n, :]
                pg.op("pe", lambda e, sl=sl, lb=lb: e.matmul(PS[7][:, sl * 128:(sl + 1) * 128], lhsT=lb, rhs=U[:], start=True, stop=False),
                      reads=["Btm", "E1", Ures], writes=psr(7, sl * 128, (sl + 1) * 128))
                pg.op("pe", lambda e, sl=sl, lk=lk: e.matmul(PS[7][:, sl * 128:(sl + 1) * 128], lhsT=lk, rhs=Vtm[:], start=False, stop=True),
                      reads=["Ktm", "E2", "Vtm"], writes=psr(7, sl * 128, (sl + 1) * 128))
                gcol = (n + 1) * L - 1
                for h in range(2):
                    hs = slice(h * 64, (h + 1) * 64)
                    dv(lambda e, hs=hs, h=h, n=n, sl=sl, gcol=gcol: e.scalar_tensor_tensor(
                        out=S[hs, n, :], in0=S[hs, n, :], scalar=G1[hs, gcol:gcol + 1],
                        in1=PS[7][hs, sl * 128 + h * 64:sl * 128 + (h + 1) * 64], op0=ALU.mult, op1=ALU.add),
                       [Sres, "G1"] + psr(7, sl * 128, (sl + 1) * 128), [Sres])
            if cfg.get('sc_stage', 9) <= 6:
                return
            pg.op("pe", lambda e: e.matmul(PS[6][:, 128:256], lhsT=BO64, rhs=Ysb[:], start=True, stop=True),
                  reads=["cm", "Ysb"], writes=psr(6, 128, 256))
            dv(lambda e: e.tensor_tensor(out=Yc[:], in0=Ysb[:], in1=PS[6][:, 128:256], op=ALU.subtract),
               ["Ysb"] + psr(6, 128, 256), ["Yc"])
            dv(lambda e: e.tensor_tensor(out=Ysq[:], in0=Yc[:], in1=Yc[:], op=ALU.mult), ["Yc"], ["Ysq"], eng="pool")
            pg.op("pe", lambda e: e.matmul(PS[6][:, 256:384], lhsT=BO64, rhs=Ysq[:], start=True, stop=True),
                  reads=["cm", "Ysq"], writes=psr(6, 256, 384))
            dv(lambda e: e.tensor_scalar(out=Yr[:], in0=PS[6][:, 256:384], scalar1=GN_EPS, scalar2=None, op0=ALU.add),
               psr(6, 256, 384), ["Yr"])
            dv(lambda e: e.activation(out=Yr[:], in_=Yr[:], func=AF.Sqrt), ["Yr"], ["Yr"], eng="act")
            dv(lambda e: e.reciprocal(out=Yr[:], in_=Yr[:]), ["Yr"], ["Yr"])
            dv(lambda e: e.tensor_tensor(out=Yc[:], in0=Yc[:], in1=Yr[:], op=ALU.mult), ["Yc", "Yr"], ["Yc"])
            dv(lambda e: e.tensor_scalar(out=Yc[:], in0=Yc[:], scalar1=pvt[:, c, 5:6], scalar2=pvt[:, c, 6:7],
                                         op0=ALU.mult, op1=ALU.add), ["Yc", "pvt"], ["Yc"])
            dv(lambda e: e.tensor_tensor(out=Yc[:], in0=Yc[:], in1=bon_[:, cs], op=ALU.add), ["Yc", bonr], ["Yc"])
            dv(lambda e: e.tensor_tensor(out=act[:, c, cs], in0=Yc[:], in1=gg_[:, cs], op=ALU.mult), ["Yc", ggr], [f"act{c}"])

        def scan_block(c, ntp, last_ti, need_out=True, hooks=(), gb=None):
            gg_, ggr, bon_, bonr = gb if gb else GB[0]
            W = ntp * 128
            v3 = lambda ap: ap[:, 0:W].rearrange("p (n t) -> p n t", t=128)
            Xreg = lambda ti, h: (4 + ti // 2, (ti % 2) * 256 + h * 128)
            sqbank = [3, 6, 7]
            dv(lambda e: e.tensor_tensor_scan(out=nLam[:, 0:W], data0=scanP[:, 0:W], data1=lgw[:, 0:W], initial=0.0,
                                              op0=ALU.mult, op1=ALU.add), ["scanP", "lgw"], ["nLam"])
            dv(lambda e: e.tensor_tensor(out=v3(nLc), in0=v3(nLam)[:, :, 127:128].to_broadcast([128, ntp, 128]), in1=v3(nLam),
                                         op=ALU.subtract), ["nLam"], ["nLc"])
            dv(lambda e: e.tensor_tensor(out=nLe[:, 0:W], in0=nLam[:, 0:W], in1=lgw[:, 0:W], op=ALU.subtract), ["nLam", "lgw"], ["nLe"],
               eng="pool")
            dv(lambda e: e.activation(out=nG1[:, 0:W], in_=nLam[:, 0:W], func=AF.Exp), ["nLam"], ["nG1"], eng="act")
            dv(lambda e: e.activation(out=nG2[:, 0:W], in_=nLam[:, 0:W], func=AF.Exp, scale=-1.0), ["nLam"], ["nG2"], eng="act")
            dv(lambda e: e.activation(out=nG3[:, 0:W], in_=nLe[:, 0:W], func=AF.Exp), ["nLe"], ["nG3"], eng="act")
            dv(lambda e: e.activation(out=nG4[:, 0:W], in_=nLc[:, 0:W], func=AF.Exp), ["nLc"], ["nG4"], eng="act")
            dv(lambda e: e.scalar_tensor_tensor(out=nAR[:, 0:ntp, 0, :], in0=v3(kk), scalar=-1.0, in1=v3(nG3),
                                                op0=ALU.mult, op1=ALU.mult), ["kk", "nG3"], ["nAR"])
            dv(lambda e: e.tensor_tensor(out=nAR[:, 0:ntp, 1, :], in0=v3(qr), in1=v3(nG1), op=ALU.mult), ["qr", "nG1"], ["nAR"])
            dv(lambda e: e.tensor_tensor(out=nbt[:, 0:W], in0=ka[:, 0:W], in1=nG2[:, 0:W], op=ALU.mult), ["ka", "nG2"], ["nbt"])
            dv(lambda e: e.tensor_tensor(out=nkt[:, 0:W], in0=kp[:, 0:W], in1=nG2[:, 0:W], op=ALU.mult), ["kp", "nG2"], ["nkt"])
            dv(lambda e: e.tensor_tensor(out=nbh[:, 0:W], in0=ka[:, 0:W], in1=nG4[:, 0:W], op=ALU.mult), ["ka", "nG4"], ["nbh"], eng="pool")
            dv(lambda e: e.tensor_tensor(out=nkh[:, 0:W], in0=kp[:, 0:W], in1=nG4[:, 0:W], op=ALU.mult), ["kp", "nG4"], ["nkh"], eng="pool")
            for ti in range(ntp):
                cs = slice(ti * 128, (ti + 1) * 128)
                for j, (src, sres) in enumerate(((qv[:, cs], "qv"), (nbh[:, cs], "nbh"), (nkh[:, cs], "nkh"))):
                    pg.op("pe", lambda e, j=j, src=src: e.transpose(out=PS[0][:, j * 128:(j + 1) * 128], in_=src, identity=ident),
                          reads=[sres, "cm"], writes=["ps0"])
                dv(lambda e, ti=ti: e.activation(out=nVBK[:, ti, :, :], in_=PS[0][:, 0:384].rearrange("p (a t) -> p a t", t=128),
                                                 func=AF.Copy), ["ps0"], ["nVBK"], eng="act")
                for h in range(2):
                    hs = slice(h * 64, (h + 1) * 64)
                    pg.op("pe", lambda e, hs=hs, h=h, ti=ti, cs=cs: e.matmul(PS[1][:, h * 256:(h + 1) * 256], lhsT=nbt[hs, cs],
                                                                                rhs=nAR[hs, ti, :, :].rearrange("p a t -> p (a t)"), start=True, stop=True),
                          reads=["nbt", "nAR"], writes=["ps1"])
                    pg.op("pe", lambda e, hs=hs, h=h, ti=ti, cs=cs: e.matmul(PS[2][:, h * 256:(h + 1) * 256], lhsT=nkt[hs, cs],
                                                                                rhs=nAR[hs, ti, :, :].rearrange("p a t -> p (a t)"), start=True, stop=True),
                          reads=["nkt", "nAR"], writes=["ps2"])
                    pg.op("pe", lambda e, hs=hs, h=h, ti=ti, cs=cs: e.matmul(PS[3][:, h * 128:(h + 1) * 128], lhsT=nAR[hs, ti, 0, :],
                                                                                rhs=nbt[hs, cs], start=True, stop=True),
                          reads=["nbt", "nAR"], writes=["ps3"])
                dv(lambda e, ti=ti: e.tensor_tensor(out=nMB[:, ti, :, :], in0=PS[1][:, 0:512].rearrange("p (a t) -> p a t", t=256),
                                                    in1=maskA[1].unsqueeze(1).to_broadcast([128, 2, 256]), op=ALU.mult),
                   ["ps1", "cm"], ["nMB"])
                dv(lambda e, ti=ti: e.tensor_tensor(out=nMK[:, ti, :, :], in0=PS[2][:, 0:512].rearrange("p (a t) -> p a t", t=256),
                                                    in1=maskA[1].unsqueeze(1).to_broadcast([128, 2, 256]), op=ALU.mult),
                   ["ps2", "cm"], ["nMK"])
                dv(lambda e, ti=ti: e.tensor_tensor(out=nPQ[0][:, ti, 0:2, :], in0=PS[3][:, 0:256].rearrange("p (a t) -> p a t", t=128),
                                                    in1=maskL[1].unsqueeze(1).to_broadcast([128, 2, 128]), op=ALU.mult),
                   ["ps3", "cm"], ["nPQ0"])
                dv(lambda e, ti=ti: e.tensor_copy(out=nPQ[0][:, ti, 2:4, :], in_=nMB[:, ti, :, 0:128]), ["nMB"], ["nPQ0"], eng="pool")
            dv(lambda e: e.memset(PS[4][:, 0:512], 0.0), [], ["ps4"])
            if ntp > 2:
                dv(lambda e: e.memset(PS[5][:, 0:256], 0.0), [], ["ps5"])
            for ti in range(ntp):
                for h in range(2):
                    hs = slice(h * 64, (h + 1) * 64)
                    xb_, xc = Xreg(ti, h)
                    pg.op("pe", lambda e, hs=hs, h=h, ti=ti, xb_=xb_, xc=xc: e.matmul(PS[xb_][:, xc:xc + 64], lhsT=nAR[hs, ti, 0, :],
                                                                                        rhs=identb[hs, h * 64:(h + 1) * 64], start=False, stop=True),
                          reads=["nAR", "identb"], writes=[f"ps{xb_}"])
                    pg.op("pe", lambda e, h=h, ti=ti, xb_=xb_, xc=xc: e.matmul(PS[xb_][:, xc + 64:xc + 128], lhsT=nMK[:, ti, h, 0:128],
                                                                                 rhs=nVBK[:, ti, 0, h * 64:(h + 1) * 64], start=False, stop=True),
                          reads=["nMK", "nVBK"], writes=[f"ps{xb_}"])

            def copy_X():
                dv(lambda e: e.activation(out=nXb[:, 0:min(ntp, 2), :, :].rearrange("p n a t -> p (n a t)"),
                                          in_=PS[4][:, 0:min(ntp, 2) * 256], func=AF.Copy), ["ps4"], ["nXb"], eng="act")
                if ntp > 2:
                    dv(lambda e: e.tensor_copy(out=nXb[:, 2, :, :].rearrange("p a t -> p (a t)"), in_=PS[5][:, 0:256]), ["ps5"], ["nXb"])

            for lev in range(7):
                pq = nPQ[lev % 2]
                pqr = f"nPQ{lev % 2}"
                if lev < len(hooks):
                    hooks[lev]()
                copy_X()
                for ti in range(ntp):
                    for h in range(2):
                        xb_, xc = Xreg(ti, h)
                        pg.op("pe", lambda e, h=h, ti=ti, xb_=xb_, xc=xc, pq=pq: e.matmul(PS[xb_][:, xc:xc + 128], lhsT=pq[:, ti, 2 + h, :],
                                                                                           rhs=nXb[:, ti, h, :], start=False, stop=True),
                              reads=[pqr, "nXb"], writes=[f"ps{xb_}"])
                if lev < 6:
                    nq = nPQ[(lev + 1) % 2]
                    nqr = f"nPQ{(lev + 1) % 2}"
                    for ti in range(ntp):
                        sb_ = sqbank[ti]
                        for h in range(2):
                            pg.op("pe", lambda e, h=h, ti=ti, sb_=sb_, pq=pq: e.matmul(PS[sb_][:, h * 128:(h + 1) * 128], lhsT=pq[:, ti, 2 + h, :],
                                                                                        rhs=pq[:, ti, h, :], start=True, stop=True),
                                  reads=[pqr], writes=[f"ps{sb_}"])
                            pg.op("pe", lambda e, h=h, ti=ti, sb_=sb_, pq=pq: e.matmul(PS[sb_][:, (2 + h) * 128:(3 + h) * 128], lhsT=pq[:, ti, h, :],
                                                                                        rhs=pq[:, ti, 2 + h, :], start=True, stop=True),
                                  reads=[pqr], writes=[f"ps{sb_}"])
                        if ti % 2 == 0:
                            dv(lambda e, ti=ti, sb_=sb_, nq=nq: e.tensor_copy(out=nq[:, ti, :, :].rearrange("p a t -> p (a t)"), in_=PS[sb_][:, 0:512]),
                               [f"ps{sb_}"], [nqr])
                        else:
                            dv(lambda e, ti=ti, sb_=sb_, nq=nq: e.activation(out=nq[:, ti, :, :].rearrange("p a t -> p (a t)"), in_=PS[sb_][:, 0:512],
                                                                              func=AF.Copy), [f"ps{sb_}"], [nqr], eng="act")
            for hk in hooks[7:]:
                hk()
            copy_X()
            if need_out:
                dv(lambda e: e.memset(PS[7][:, 0:W], 0.0), [], ["ps7"])
            for ti in range(ntp):
                cs = slice(ti * 128, (ti + 1) * 128)
                for h in range(2):
                    hs = slice(h * 64, (h + 1) * 64)
                    if need_out:
                        pg.op("pe", lambda e, hs=hs, h=h, ti=ti, cs=cs: e.matmul(PS[7][hs, cs], lhsT=nXb[:, ti, h, 0:64], rhs=nMB[:, ti, h, 128:256],
                                                                                    start=False, stop=False), reads=["nXb", "nMB"], writes=["ps7"])
                        pg.op("pe", lambda e, hs=hs, h=h, ti=ti, cs=cs: e.matmul(PS[7][hs, cs], lhsT=identb[hs, h * 64:(h + 1) * 64], rhs=nAR[hs, ti, 1, :],
                                                                                    start=False, stop=True), reads=["identb", "nAR"], writes=["ps7"])
                    pg.op("pe", lambda e, hs=hs, h=h, ti=ti: e.matmul(PS[3][hs, ti * 64:(ti + 1) * 64], lhsT=nXb[:, ti, h, 0:64],
                                                                       rhs=nVBK[:, ti, 1, h * 64:(h + 1) * 64], start=True, stop=True),
                          reads=["nXb", "nVBK"], writes=["ps3"])
            if need_out:
                dv(lambda e: e.activation(out=nRH[:, 0:W], in_=PS[7][:, 0:W], func=AF.Copy), ["ps7"], ["nRH"], eng="act")
            for ti in range(ntp):
                gc = ti * 128 + 127
                dv(lambda e, ti=ti, gc=gc: e.scalar_tensor_tensor(out=nMS[:, ti, :], in0=I2[:], scalar=nG1[:, gc:gc + 1],
                                                                  in1=PS[3][:, ti * 64:(ti + 1) * 64], op0=ALU.mult, op1=ALU.add),
                   ["I2", "nG1", "ps3"], ["nMS"])
            dv(lambda e: e.memset(PS[6][:, 0:ntp * 64], 0.0), [], ["ps6"])
            if need_out:
                dv(lambda e: e.memset(PS[0][:, 0:W], 0.0), [], ["ps0"])
            for ti in range(ntp):
                cs = slice(ti * 128, (ti + 1) * 128)
                for h in range(2):
                    hs = slice(h * 64, (h + 1) * 64)
                    hc = slice(h * 64, (h + 1) * 64)
                    pg.op("pe", lambda e, hs=hs, hc=hc, h=h, ti=ti: e.matmul(PS[6][hs, ti * 64:(ti + 1) * 64], lhsT=nVBK[:, ti, 1, hc],
                                                                                rhs=nXb[:, ti, h, 64:128], start=False, stop=False),
                          reads=["nVBK", "nXb"], writes=["ps6"])
                    pg.op("pe", lambda e, hs=hs, hc=hc, h=h, ti=ti: e.matmul(PS[6][hs, ti * 64:(ti + 1) * 64], lhsT=nVBK[:, ti, 2, hc],
                                                                                rhs=nVBK[:, ti, 0, hc], start=False, stop=False),
                          reads=["nVBK"], writes=["ps6"])
                    if need_out:
                        pg.op("pe", lambda e, hs=hs, hc=hc, h=h, ti=ti, cs=cs: e.matmul(PS[0][hs, cs], lhsT=nXb[:, ti, h, 64:128],
                                                                                           rhs=nMB[:, ti, h, 128:256], start=False, stop=False),
                              reads=["nXb", "nMB"], writes=["ps0"])
                        pg.op("pe", lambda e, hs=hs, hc=hc, h=h, ti=ti, cs=cs: e.matmul(PS[0][hs, cs], lhsT=nVBK[:, ti, 0, hc],
                                                                                           rhs=nMK[:, ti, h, 128:256], start=False, stop=False),
                              reads=["nVBK", "nMK"], writes=["ps0"])
            for ti in range(ntp):
                cs = slice(ti * 128, (ti + 1) * 128)
                for h in range(2):
                    if not need_out:
                        break
                    hs = slice(h * 64, (h + 1) * 64)
                    pg.op("pe", lambda e, hs=hs, cs=cs: e.matmul(PS[0][hs, cs], lhsT=Sb[hs, c, :], rhs=nRH[hs, cs], start=False, stop=True),
                          reads=[f"Sb{c}", "nRH"], writes=["ps0"])
                for h in range(2):
                    hs = slice(h * 64, (h + 1) * 64)
                    pg.op("pe", lambda e, hs=hs, ti=ti: e.matmul(PS[6][hs, ti * 64:(ti + 1) * 64], lhsT=nMS[hs, ti, :], rhs=Sb[hs, c, :],
                                                                  start=False, stop=True),
                          reads=[f"Sb{c}", "nMS"], writes=["ps6"])
                dv(lambda e, ti=ti: e.activation(out=Sb[:, c, :], in_=PS[6][:, ti * 64:(ti + 1) * 64], func=AF.Copy), ["ps6"], [f"Sb{c}"],
                   eng="act")
                if ti == last_ti:
                    dv(lambda e, ti=ti: e.tensor_copy(out=Sp[:, c, :], in_=PS[6][:, ti * 64:(ti + 1) * 64]), ["ps6"], [f"Sp{c}"])
            if not need_out:
                return
            Yb, Ycb, Yqb, Yrb = nLc, nLe, nG2, nG3
            dv(lambda e: e.activation(out=Yb[:, 0:W], in_=PS[0][:, 0:W], func=AF.Copy), ["ps0"], ["nLc"], eng="act")
            pg.op("pe", lambda e: e.matmul(PS[1][:, 0:W], lhsT=BO64, rhs=Yb[:, 0:W], start=True, stop=True), reads=["cm", "nLc"], writes=["ps1"])
            dv(lambda e: e.tensor_tensor(out=Ycb[:, 0:W], in0=Yb[:, 0:W], in1=PS[1][:, 0:W], op=ALU.subtract), ["nLc", "ps1"], ["nLe"])
            dv(lambda e: e.tensor_tensor(out=Yqb[:, 0:W], in0=Ycb[:, 0:W], in1=Ycb[:, 0:W], op=ALU.mult), ["nLe"], ["nG2"], eng="pool")
            pg.op("pe", lambda e: e.matmul(PS[2][:, 0:W], lhsT=BO64, rhs=Yqb[:, 0:W], start=True, stop=True), reads=["cm", "nG2"], writes=["ps2"])
            dv(lambda e: e.tensor_scalar(out=Yrb[:, 0:W], in0=PS[2][:, 0:W], scalar1=GN_EPS, scalar2=None, op0=ALU.add), ["ps2"], ["nG3"])
            dv(lambda e: e.activation(out=Yrb[:, 0:W], in_=Yrb[:, 0:W], func=AF.Sqrt), ["nG3"], ["nG3"], eng="act")
            dv(lambda e: e.reciprocal(out=Yrb[:, 0:W], in_=Yrb[:, 0:W]), ["nG3"], ["nG3"])
            dv(lambda e: e.tensor_tensor(out=Ycb[:, 0:W], in0=Ycb[:, 0:W], in1=Yrb[:, 0:W], op=ALU.mult), ["nLe", "nG3"], ["nLe"])
            dv(lambda e: e.tensor_scalar(out=Ycb[:, 0:W], in0=Ycb[:, 0:W], scalar1=pvt[:, c, 5:6], scalar2=pvt[:, c, 6:7],
                                         op0=ALU.mult, op1=ALU.add), ["nLe", "pvt"], ["nLe"])
            dv(lambda e: e.tensor_tensor(out=Ycb[:, 0:W], in0=Ycb[:, 0:W], in1=bon_[:, 0:W], op=ALU.add), ["nLe", bonr], ["nLe"])
            dv(lambda e: e.tensor_tensor(out=act[:, c, 0:W], in0=Ycb[:, 0:W], in1=gg_[:, 0:W], op=ALU.mult), ["nLe", ggr], [f"act{c}"])

        def state_out(c, nseq, dst):
            S = Sp[:, c:c + 1, :] if nseq == 1 else Ss[:, :, :]
            Sres = f"Sp{c}" if nseq == 1 else "Ss"
            for n in range(nseq):
                sl = n % 4
                pg.op("pe", lambda e, n=n, sl=sl: e.transpose(out=PS[7][0:64, sl * 128:(sl + 1) * 128], in_=S[:, n, :], identity=ident),
                      reads=[Sres, "cm"], writes=psr(7, sl * 128, (sl + 1) * 128))
                dv(lambda e, n=n, sl=sl: e.tensor_copy(out=Sst[:, n, :], in_=PS[7][0:64, sl * 128:(sl + 1) * 128]),
                   psr(7, sl * 128, (sl + 1) * 128), ["E2"])
            if nseq == 1:
                pg.dma("sp", dst[2 * c:2 * c + 2, :, :].rearrange("h i j -> i h j"),
                       Sst[:, 0, :].rearrange("p (h j) -> p h j", h=2), reads=["E2"])
            else:
                for h in range(2):
                    pg.dma("sp", dst[:, 2 * c + h, :, :].rearrange("n i j -> i n j"),
                           Sst[:, :, h * 64:(h + 1) * 64], reads=["E2"])

        def state_in(c):
            for h in range(2):
                pg.dma("sp", Sld[:, :, h * 64:(h + 1) * 64],
                       s_wkv[:, 2 * c + h, :, :].rearrange("n i j -> i n j"), writes=["E1"])
            for n in range(NSQ):
                sl = n % 4
                pg.op("pe", lambda e, n=n, sl=sl: e.transpose(out=PS[7][:, sl * 128:sl * 128 + 64], in_=Sld[:, n, :], identity=ident[0:64, 0:64]),
                      reads=["E1", "cm"], writes=psr(7, sl * 128, (sl + 1) * 128))
                dv(lambda e, n=n, sl=sl: e.tensor_copy(out=Ss[:, n, :], in_=PS[7][:, sl * 128:sl * 128 + 64]),
                   psr(7, sl * 128, (sl + 1) * 128), ["Ss"])

        def mixer(blk):
            tiles = list(range(blk * NT, (blk + 1) * NT))
            has_sample = SAMPLE_TILE in tiles
            last_prompt = LAST_PROMPT_TILE in tiles
            npc = (NT - 1) * 128 if has_sample else N
            need_out = blk >= POST_BLK0
            need_carry = blk >= POST_BLK0 - 1
            for kind, cid, bank in (("wd", 8, 0), ("ad", 25, 1), ("gd0", 26, 2), ("gd1", 27, 3)):
                if kind.startswith("gd") and not need_carry:
                    continue
                off, w = CH_OFF[cid]
                proj("w_in", off, w, bank)
                lerp_chunk(cid, bank, pb[0], "pb0", t2, "t2", has_sample, last_prompt, npc)
                if kind == "wd":
                    dv(lambda e: e.activation(out=twd[:], in_=t2[0:96, :], func=AF.Tanh), ["t2"], ["twd"], eng="act")
                elif kind == "ad":
                    dv(lambda e: e.tensor_copy(out=qad[:], in_=t2[0:96, :]), ["t2"], ["qad"])
                elif need_out:
                    j = cid - 26
                    dv(lambda e, j=j: e.activation(out=sgd[:, j, :], in_=t2[:, :], func=AF.Sigmoid), ["t2"], ["sgd"], eng="act")
            def prep_chunks(c):
                gg_, ggr, bon_, bonr = GB[c % 2]

                def pj(j, kind, q, qres):
                    def f():
                        if j == 0:
                            convert_slot(blk, c)
                        cid = chunk_id(kind, c)
                        off, w = CH_OFF[cid]
                        proj("w_in", off, w, j)
                        lerp_chunk(cid, j, pb[j], f"pb{j}", q, qres, has_sample, last_prompt, npc)
                    return f

                def loras():
                    pg.op("pe", lambda e: e.matmul(PS[0][:, 0:N], lhsT=lww[:, c * 128:(c + 1) * 128], rhs=twd[:], start=True, stop=True),
                          reads=["lww", "twd"], writes=psr(0, 0, N))
                    dv(lambda e: e.activation(out=lgw[:], in_=PS[0][:, 0:N], func=AF.Sigmoid, bias=pvt[:, c, 0:1]),
                       psr(0, 0, N) + ["pvt"], ["lgw"], eng="act")
                    dv(lambda e: e.tensor_scalar(out=lgw[:], in0=lgw[:], scalar1=-0.6065306597126334, scalar2=None, op0=ALU.mult),
                       ["lgw"], ["lgw"], eng="pool")
                    pg.op("pe", lambda e: e.matmul(PS[1][:, 0:N], lhsT=lwa[:, c * 128:(c + 1) * 128], rhs=qad[:], start=True, stop=True),
                          reads=["lwa", "qad"], writes=psr(1, 0, N))
                    dv(lambda e: e.activation(out=av[:], in_=PS[1][:, 0:N], func=AF.Sigmoid, bias=pvt[:, c, 1:2]),
                       psr(1, 0, N) + ["pvt"], ["av"], eng="act")
                    if need_out:
                        for j in range(2):
                            pg.op("pe", lambda e, j=j: e.matmul(PS[2][:, 0:N], lhsT=lwg[:, j, c * 128:(c + 1) * 128], rhs=sgd[:, j, :],
                                                                start=(j == 0), stop=(j == 1)),
                                  reads=["lwg", "sgd"], writes=psr(2, 0, N))
                        dv(lambda e: e.activation(out=gg_[:], in_=PS[2][:, 0:N], func=AF.Copy), psr(2, 0, N), [ggr], eng="act")

                def kknorm():
                    dv(lambda e: e.tensor_scalar(out=kk[:], in0=qk[:], scalar1=pvt[:, c, 2:3], scalar2=None, op0=ALU.mult),
                       ["qk", "pvt"], ["kk"])
                    dv(lambda e: e.tensor_tensor(out=t1[:], in0=kk[:], in1=kk[:], op=ALU.mult), ["kk"], ["t1"])
                    pg.op("pe", lambda e: e.matmul(PS[0][:, 0:N], lhsT=BO, rhs=t1[:], start=True, stop=True),
                          reads=["cm", "t1"], writes=psr(0, 0, N))
                    dv(lambda e: e.activation(out=t2[:], in_=PS[0][:, 0:N], func=AF.Sqrt), psr(0, 0, N), ["t2"], eng="act")
                    dv(lambda e: e.tensor_scalar(out=t2[:], in0=t2[:], scalar1=1e-12, scalar2=None, op0=ALU.max), ["t2"], ["t2"])
                    dv(lambda e: e.reciprocal(out=t2[:], in_=t2[:]), ["t2"], ["t2"])
                    dv(lambda e: e.tensor_tensor(out=kk[:], in0=kk[:], in1=t2[:], op=ALU.mult), ["kk", "t2"], ["kk"])

                def kpbon():
                    dv(lambda e: e.tensor_scalar(out=kp[:], in0=av[:], scalar1=pvt[:, c, 3:4], scalar2=pvt[:, c, 7:8],
                                                 op0=ALU.mult, op1=ALU.add), ["av", "pvt"], ["kp"])
                    dv(lambda e: e.tensor_tensor(out=kp[:], in0=kp[:], in1=qk[:], op=ALU.mult), ["kp", "qk"], ["kp"])
                    dv(lambda e: e.tensor_tensor(out=ka[:], in0=kk[:], in1=av[:], op=ALU.mult), ["kk", "av"], ["ka"], eng="pool")
                    if need_out:
                        dv(lambda e: e.scalar_tensor_tensor(out=t1[:], in0=qr[:], scalar=pvt[:, c, 4:5], in1=kp[:],
                                                            op0=ALU.mult, op1=ALU.mult), ["qr", "kp", "pvt"], ["t1"])
                        pg.op("pe", lambda e: e.matmul(PS[1][:, 0:N], lhsT=BO, rhs=t1[:], start=True, stop=True),
                              reads=["cm", "t1"], writes=psr(1, 0, N))
                        dv(lambda e: e.tensor_tensor(out=bon_[:], in0=qv[:], in1=PS[1][:, 0:N], op=ALU.mult),
                           ["qv"] + psr(1, 0, N), [bonr])

                return [pj(0, "r", qr, "qr"), pj(1, "k", qk, "qk"), pj(2, "v", qv, "qv"), loras, kknorm, kpbon]

            interleave = cfg.get("interleave", True) and not has_sample
            if cfg.get("mx_stage", 9) >= 3:
                ntp = NT - 1 if has_sample else NT
                last_ti = tiles.index(LAST_PROMPT_TILE) if last_prompt else -1
                for ch in prep_chunks(0):
                    ch()
                for c in range(8):
                    if c > 0 and not interleave:
                        for ch in prep_chunks(c):
                            ch()
                    hooks = prep_chunks(c + 1) if (interleave and c < 7) else ()
                    scan_block(c, ntp, last_ti, need_out, hooks, GB[c % 2])
                    if has_sample:
                        pg.barrier()
                        if last_prompt:
                            state_out(c, 1, wkv_p)
                        state_in(c)
                        scan_tile(c, npc, NSQ, False, GB[c % 2])
                        state_out(c, NSQ, wkv_s)
                        pg.barrier()
            for cc in range(8):
                if cfg.get("mx_stage", 9) < 4:
                    break
                if not need_carry:
                    break
                if need_out:
                    proj("w_in", OFF_CB + cc * 128, 128, 0)
                proj("w_in", OFF_CC + cc * 128, 128, 1)
                proj("w_in", OFF_CX + cc * 128, 128, 2)
                dv(lambda e, cc=cc: e.tensor_copy(out=ub[:, 0:2], in_=ucar[:, cc, :]), ["ucar"], ["ub"])
                dv(lambda e: e.activation(out=t1[:], in_=PS[1][:, 0:N], func=AF.Copy), psr(1, 0, N), ["t1"], eng="act")
                dv(lambda e: e.tensor_tensor(out=ub[:, 2:N + 2], in0=t1[:], in1=PS[2][:, 0:N], op=ALU.mult),
                   ["t1"] + psr(2, 0, N), ["ub"])
                if npc > 0:
                    dv(lambda e, cc=cc: e.tensor_copy(out=ucar[:, cc, :], in_=ub[:, npc:npc + 2]), ["ub"], ["ucar"])
                if not need_out:
                    continue
                dv(lambda e, cc=cc: e.tensor_scalar(out=cvt[:], in0=ub[:, 0:N], scalar1=cwt[:, cc, 0:1], scalar2=None, op0=ALU.mult),
                   ["ub", "cwt"], ["t2"])
                dv(lambda e, cc=cc: e.scalar_tensor_tensor(out=cvt[:], in0=ub[:, 1:N + 1], scalar=cwt[:, cc, 1:2], in1=cvt[:],
                                                           op0=ALU.mult, op1=ALU.add), ["ub", "cwt", "t2"], ["t2"])
                dv(lambda e, cc=cc: e.scalar_tensor_tensor(out=cvt[:], in0=ub[:, 2:N + 2], scalar=cwt[:, cc, 2:3], in1=cvt[:],
                                                           op0=ALU.mult, op1=ALU.add), ["ub", "cwt", "t2"], ["t2"])
                if last_prompt:
                    pg.dma("sp", conv_p[:, cc * 128:(cc + 1) * 128].rearrange("t c -> c t"), ub[:, npc:npc + 2], reads=["ub"])
                if has_sample:
                    k = cc % 2
                    pg.dma("sp", stg[k][:, :], s_conv.rearrange("n t c -> (n t) c")[:, cc * 128:(cc + 1) * 128], writes=[f"stg{k}"])
                    pg.op("pe", lambda e, k=k: e.transpose(out=PS[7][:, k * 128:k * 128 + 32], in_=stg[k][:, :], identity=ident[0:32, 0:32]),
                          reads=[f"stg{k}", "cm"], writes=["ps7"])
                    dv(lambda e, k=k: e.tensor_copy(out=uS[:, :, 0:2], in_=PS[7][:, k * 128:k * 128 + 32].rearrange("p (n t) -> p n t", t=2)),
                       ["ps7"], ["uS"])
                    dv(lambda e: e.tensor_copy(out=uS[:, :, 2:LSQ + 2], in_=ub[:, npc + 2:npc + 130].rearrange("p (n t) -> p n t", t=LSQ)),
                       ["ub"], ["uS"])
                    cS = cvt[:, npc:npc + 128].rearrange("p (n t) -> p n t", t=LSQ)
                    dv(lambda e, cc=cc: e.tensor_scalar(out=cS, in0=uS[:, :, 0:LSQ], scalar1=cwt[:, cc, 0:1], scalar2=None, op0=ALU.mult),
                       ["uS", "cwt", "t2"], ["t2"])
                    dv(lambda e, cc=cc: e.scalar_tensor_tensor(out=cS, in0=uS[:, :, 1:LSQ + 1], scalar=cwt[:, cc, 1:2], in1=cS,
                                                               op0=ALU.mult, op1=ALU.add), ["uS", "cwt", "t2"], ["t2"])
                    dv(lambda e, cc=cc: e.scalar_tensor_tensor(out=cS, in0=uS[:, :, 2:LSQ + 2], scalar=cwt[:, cc, 2:3], in1=cS,
                                                               op0=ALU.mult, op1=ALU.add), ["uS", "cwt", "t2"], ["t2"])
                    k = 2 + cc % 2
                    dv(lambda e: e.tensor_copy(out=ctmp.rearrange("p (n t) -> p n t", t=2), in_=uS[:, :, LSQ:LSQ + 2]), ["uS"], ["ctmp"])
                    pg.op("pe", lambda e, k=k: e.transpose(out=PS[6][0:32, k * 128:(k + 1) * 128], in_=ctmp, identity=ident),
                          reads=["ctmp", "cm"], writes=["ps6"])
                    dv(lambda e, k=k: e.tensor_copy(out=stg[k][:, :], in_=PS[6][0:32, k * 128:(k + 1) * 128]), ["ps6"], [f"stg{k}"])
                    pg.dma("sp", conv_s.rearrange("n t c -> (n t) c")[:, cc * 128:(cc + 1) * 128], stg[k][:, :], reads=[f"stg{k}"])
                dv(lambda e, cc=cc: e.tensor_tensor(out=act[:, 8 + cc, :], in0=cvt[:], in1=PS[0][:, 0:N], op=ALU.mult),
                   ["t2"] + psr(0, 0, N), [f"act{8 + cc}"])
            if last_prompt:
                for cid, (off, w) in CH_OFF.items():
                    pg.dma("sp", shift_p[off:off + w].rearrange("(c o) -> c o", o=1), shP[0:w, cid:cid + 1], reads=["shP"])
            if has_sample:
                for cid, (off, w) in CH_OFF.items():
                    k = cid % 4
                    pg.op("pe", lambda e, k=k, w=w, cid=cid: e.transpose(out=PS[7][0:NSQ, k * 128:k * 128 + w], in_=shO[0:w, cid, :],
                                                                         identity=ident[0:w, 0:w]), reads=["shO", "cm"], writes=["ps7"])
                    dv(lambda e, k=k, w=w: e.tensor_copy(out=stg[k][0:NSQ, 0:w], in_=PS[7][0:NSQ, k * 128:k * 128 + w]), ["ps7"], [f"stg{k}"])
                    pg.dma("sp", shift_s[:, off:off + w], stg[k][0:NSQ, 0:w], reads=[f"stg{k}"])

        def emit():
            wnext["w"] = 0
            wnext["d"] = 0
            wdone.clear()
            conv_pos.clear()
            pg.dma("sp", cm[:], cmat, writes=["cm"])
            pg.dma("sp", cmask[:].rearrange("p n t -> p (n t)"), cmask_d, writes=["cmask"])
            pg.dma("sp", rmask[:], rmask_d, writes=["rmask"])
            pg.dma("sp", gv[:], gvec, writes=["gv"])
            pg.dma("sp", gf[:], gfin, writes=["gf"])
            pg.dma("sp", mu[:], mu_fm, writes=["mu"])
            pg.dma("sp", pvt[:], pv, writes=["pvt"])
            pg.dma("sp", cwt[:], cw, writes=["cwt"])
            pg.dma("pool", lww[:], lw_w, writes=["lww"])
            pg.dma("pool", lwa[:], lw_a, writes=["lwa"])
            pg.dma("pool", lwg[:], lw_g.rearrange("(j p) c -> p j c", p=128), writes=["lwg"])
            for i, (cid, (off, w)) in enumerate(CH_OFF.items()):
                k = i % 4
                pg.dma("sp", stg[k][0:NSQ, 0:w], s_shift[:, off:off + w], writes=[f"stg{k}"])
                pg.op("pe", lambda e, k=k, w=w: e.transpose(out=PS[7][0:w, k * 128:k * 128 + NSQ], in_=stg[k][0:NSQ, 0:w],
                                                            identity=ident[0:NSQ, 0:NSQ]), reads=[f"stg{k}", "cm"], writes=["ps7"])
                dv(lambda e, k=k, w=w, cid=cid: e.tensor_copy(out=shT[0:w, cid, :], in_=PS[7][0:w, k * 128:k * 128 + NSQ]),
                   ["ps7"], ["shT"])
            pg.barrier()
            dv(lambda e: e.tensor_scalar(out=omu[:], in0=mu[:], scalar1=-1.0, scalar2=1.0, op0=ALU.mult, op1=ALU.add),
               ["mu"], ["omu"])
            dv(lambda e: e.memset(carry[:], 0.0), [], ["carry"], eng="pool")
            dv(lambda e: e.tensor_copy(out=identb[:], in_=ident), ["cm"], ["identb"])
            dv(lambda e: e.tensor_tensor(out=I2[:], in0=cm[:, 0:64], in1=cm[:, 64:128], op=ALU.add), ["cm"], ["I2"])
            dv(lambda e: e.memset(scanP[:], 1.0), [], ["scanP"], eng="pool")
            dv(lambda e: e.memset(scanP[:].rearrange("p (n t) -> p n t", t=128)[:, :, 0:1], 0.0), [], ["scanP"], eng="pool")
            dv(lambda e: e.memset(Sb[:].rearrange("p a b -> p (a b)"), 0.0), [], [f"Sb{c}" for c in range(8)], eng="pool")
            dv(lambda e: e.memset(ucar[:].rearrange("p a b -> p (a b)"), 0.0), [], ["ucar"], eng="pool")
            dv(lambda e: e.memset(Sp[:].rearrange("p a b -> p (a b)"), 0.0), [], [f"Sp{c}" for c in range(8)], eng="pool")
            dv(lambda e: e.memset(stat[:], 0.0), [], ["stat0", "stat1", "stat2"], eng="pool")
            for blk in range(NBLK):
                post = blk >= POST_BLK0
                for tt in range(NT):
                    tile = blk * NT + tt
                    pg.dma("sp", xb[tt][:], xw[tile * 128:(tile + 1) * 128, :], writes=[f"xb{tt}"])
                if blk >= cfg.get("nblk", NBLK):
                    continue
                norm_T(0)
                if cfg.get("ffn", True):
                    ffn("w1g", "w1u", "w1d")
                norm_T(KC)
                if cfg.get("mixer", True):
                    mixer(blk)
                if post and cfg.get("post", True):
                    down_proj(KC, "w_out", 0, 1.0, list(range(NT)))
                    norm_T(2 * KC)
                    ffn("w2g", "w2u", "w2d")
                    for tt in range(NT):
                        xr = f"xb{tt}"
                        pg.op("act", lambda e, tt=tt: e.activation(out=junk[:], in_=xb[tt][:], func=AF.Square, accum_out=stat[:, 0:1]),
                              reads=[xr], writes=JUNK + ["stat0"])
                        pg.op("dve", lambda e: e.tensor_scalar(out=stat[:, 1:2], in0=stat[:, 0:1], scalar1=1.0 / D, scalar2=RMS_EPS,
                                                               op0=ALU.mult, op1=ALU.add), reads=["stat0"], writes=["stat1"])
                        pg.op("act", lambda e: e.activation(out=stat[:, 1:2], in_=stat[:, 1:2], func=AF.Sqrt), reads=["stat1"], writes=["stat1"])
                        pg.op("dve", lambda e: e.reciprocal(out=stat[:, 2:3], in_=stat[:, 1:2]), reads=["stat1"], writes=["stat2"])
                        pg.op("dve", lambda e, tt=tt: e.scalar_tensor_tensor(out=xb[tt][:], in0=xb[tt][:], scalar=stat[:, 2:3], in1=gf[:],
                                                                             op0=ALU.mult, op1=ALU.mult),
                              reads=[xr, "stat2", "gf"], writes=[xr])
                        ot = (blk - POST_BLK0) * NT + tt
                        pg.dma("sp", y_out[ot * 128:(ot + 1) * 128, :], xb[tt][:], reads=[xr])
            pg.finish()

        pg.dry = True
        emit()
        pg.dry = False
        emit()
    return nc


_CACHE = {}
_DEBUG = {}


def _consts():
    cmx = np.zeros((128, 2048), np.float32)
    cmx[:, 0:128] = np.eye(128)
    bo = np.kron(np.eye(2), np.ones((64, 64))).astype(np.float32)
    cmx[:, 128:256] = bo
    cmx[:, 256:384] = bo / 64.0
    s = np.arange(128)[:, None]
    t = np.arange(128)[None, :]
    for nseq, a0, l0, sc0 in ((1, 384, 640, 1152), (NSQ, 768, 1024, 1280)):
        L = 128 // nseq
        same = (s // L) == (t // L)
        MsT = ((s < t) & same).astype(np.float32)
        MiT = ((s <= t) & same).astype(np.float32)
        cmx[:, a0:a0 + 128] = MsT
        cmx[:, a0 + 128:a0 + 256] = MiT
        cmx[:, l0:l0 + 128] = MsT.T
        sm = np.ones((128, 128), np.float32)
        sm[:, (np.arange(128) % L) == 0] = 0.0
        cmx[:, sc0:sc0 + 128] = sm
    cmask = np.zeros((128, NSQ, 128), np.float32)
    for n in range(NSQ):
        cmask[:, n, n * LSQ:(n + 1) * LSQ] = 1.0
    rmask = np.zeros((128, NSQ), np.float32)
    for n in range(NSQ):
        rmask[n * LSQ:(n + 1) * LSQ, n] = 1.0
    return cmx, cmask.reshape(128, NSQ * 128), rmask


def kernel(x_prompt, x_sample, state_wkv, state_shift, state_conv, meta_tokens,
           g_ffn1, ffn1_gate, ffn1_up, ffn1_down, g_mix, w_in, mu_shift, w0, w_lora_w,
           a0, w_lora_a, w_lora_g, k_k, k_a, r_k, ln_x_w, ln_x_b, conv_w, w_out,
           g_ffn2, ffn2_gate, ffn2_up, ffn2_down, g_final):
    f = lambda a: np.ascontiguousarray(np.asarray(a, dtype=np.float32))
    x_prompt, x_sample = f(x_prompt), f(x_sample)
    if "nc" not in _CACHE:
        _CACHE["nc"] = build_program(_DEBUG.get("cfg"))
    nc = _CACHE["nc"]
    cmx, cmask, rmask = _consts()
    fm = lambda v: np.ascontiguousarray(f(v).reshape(-1, 128).T)
    gvec = np.concatenate([fm(g_ffn1[0]), fm(g_mix[0]), fm(g_ffn2[0])], axis=1)
    gfin = np.ascontiguousarray(np.broadcast_to(f(g_final)[None, :], (128, D)))
    mu_fm = np.zeros((128, 28), np.float32)
    mus = f(mu_shift[0])
    offs = {}
    for c in range(8):
        offs[c] = (OFF_R + c * 128, 128); offs[9 + c] = (OFF_K + c * 128, 128); offs[17 + c] = (OFF_V + c * 128, 128)
    offs[8] = (OFF_WD, 96); offs[25] = (OFF_AD, 96); offs[26] = (OFF_GD, 128); offs[27] = (OFF_GD + 128, 128)
    for cid, (o, w) in offs.items():
        mu_fm[0:w, cid] = mus[o:o + w]
    pvv = np.zeros((128, 8, 8), np.float32)
    for i, v in enumerate((w0[0], a0[0], k_k[0], k_a[0], f(r_k[0]).reshape(-1), ln_x_w[0], ln_x_b[0])):
        pvv[:, :, i] = f(v).reshape(8, 128).T
    pvv[:, :, 7] = 1.0 - pvv[:, :, 3]
    cwv = np.ascontiguousarray(f(conv_w[0]).reshape(3, 8, 128).transpose(2, 1, 0))
    shared = {
        "w1g": f(ffn1_gate[0]), "w1u": f(ffn1_up[0]), "w1d": f(ffn1_down[0]),
        "w2g": f(ffn2_gate[0]), "w2u": f(ffn2_up[0]), "w2d": f(ffn2_down[0]),
        "w_in": f(w_in[0]), "w_out": f(w_out[0]),
        "lw_w": f(w_lora_w[0]), "lw_a": f(w_lora_a[0]), "lw_g": f(w_lora_g[0]),
        "gvec": gvec, "gfin": gfin, "mu_fm": mu_fm, "pv": pvv, "cw": cwv,
        "cmat": cmx, "cmask": cmask, "rmask": rmask,
    }
    meta = f(meta_tokens)
    in_maps = []
    for core in range(8):
        b, half = core // 2, core % 2
        xw = np.zeros((NTILES * 128, D), np.float32)
        if half == 0:
            xw[8 * 128 + 112:9 * 128] = meta
            xw[9 * 128:17 * 128] = x_prompt[b, 0:1024]
        else:
            xw[112:128] = meta
            xw[128:17 * 128] = x_prompt[b]
        xw[17 * 128:] = x_sample[core * NSQ:(core + 1) * NSQ].reshape(128, D)
        m = dict(shared)
        m["xw"] = xw
        m["s_wkv"] = f(state_wkv[0, core * NSQ:(core + 1) * NSQ])
        m["s_shift"] = f(state_shift[0, core * NSQ:(core + 1) * NSQ])
        m["s_conv"] = f(state_conv[0, core * NSQ:(core + 1) * NSQ])
        in_maps.append(m)
    if "cores" in _DEBUG:
        sel = _DEBUG["cores"]
        res = run_bass_kernel_spmd(nc, [in_maps[i] for i in sel], core_ids=list(range(len(sel))))
        R = [res.results[sel.index(i)] if i in sel else None for i in range(8)]
    else:
        res = run_bass_kernel_spmd(nc, in_maps, core_ids=list(range(8)))
        R = res.results
    B = x_prompt.shape[0]
    y_prompt = np.zeros((B, 2048, D), np.float32)
    y_sample = np.zeros((128, 8, D), np.float32)
    wkv_p = np.zeros((1, B, NH, HD, HD), np.float32)
    shift_p = np.zeros((1, B, RWKV_PROJ), np.float32)
    conv_p = np.zeros((1, B, 2, GR), np.float32)
    wkv_s = np.zeros((1, 128, NH, HD, HD), np.float32)
    shift_s = np.zeros((1, 128, RWKV_PROJ), np.float32)
    conv_s = np.zeros((1, 128, 2, GR), np.float32)
    for core in range(8):
        b, half = core // 2, core % 2
        r = R[core]
        if r is None:
            continue
        y_prompt[b, half * 1024:(half + 1) * 1024] = r["y_out"][0:1024]
        y_sample[core * NSQ:(core + 1) * NSQ] = r["y_out"][1024:1152].reshape(NSQ, 8, D)
        if half == 1:
            wkv_p[0, b] = r["wkv_p"]
            shift_p[0, b] = r["shift_p"]
            conv_p[0, b] = r["conv_p"]
        wkv_s[0, core * NSQ:(core + 1) * NSQ] = r["wkv_s"]
        shift_s[0, core * NSQ:(core + 1) * NSQ] = r["shift_s"]
        conv_s[0, core * NSQ:(core + 1) * NSQ] = r["conv_s"]
    return (y_prompt, y_sample, wkv_p, shift_p, conv_p, wkv_s, shift_s, conv_s)
```

```python
import numpy as np
from contextlib import ExitStack
import concourse.bass as bass
import concourse.mybir as mybir
from concourse.bass_utils import run_bass_kernel_spmd

F32 = mybir.dt.float32
BF16 = mybir.dt.bfloat16
AF = mybir.ActivationFunctionType
ALU = mybir.AluOpType

D = 2048
DFF = 5632
NF = DFF // 128
KC = D // 128
NH = 16
HD = 64
GR = 1024
RWKV_PROJ = 3520
P_TOTAL = 6592
N_META = 16
RMS_EPS = 1e-6
GN_EPS = 64e-5
NT = 3
N = NT * 128
NTILES = 18
NBLK = NTILES // NT
POST_BLK0 = 3
NPOST = (NBLK - POST_BLK0) * NT
SAMPLE_TILE = NTILES - 1
LAST_PROMPT_TILE = NTILES - 2
NSQ = 16
LSQ = 8
OFF_R, OFF_WD, OFF_K, OFF_V, OFF_AD, OFF_GD = 0, 1024, 1120, 2144, 3168, 3264
OFF_CB, OFF_CC, OFF_CX = 3520, 4544, 5568
WSL = 4
DSL = 3
PREFETCH_W = 2
PREFETCH_D = 2
FSPLIT = 2
NFH = NF // FSPLIT


class Prog:
    def __init__(self, nc, es):
        self.nc = nc
        self.dry = False
        self.engs = {"pe": nc.tensor, "dve": nc.vector, "act": nc.scalar, "pool": nc.gpsimd, "sp": nc.sync}
        self.sem = {k: es.enter_context(nc.semaphore("s_" + k)) for k in self.engs}
        self.cnt = {k: 0 for k in self.engs}
        self.waited = {k: {} for k in self.engs}
        self.last_w = {}
        self.readers = {}
        self.NS = 8
        self.dsem = {q: [es.enter_context(nc.semaphore(f"d_{q}{i}")) for i in range(self.NS)] for q in ("sp", "pool")}
        self.dcnt = {q: 0 for q in ("sp", "pool")}

    def _wait(self, eng, sem, val):
        if eng == "pe" and sem is self.sem["pe"]:
            return
        key = id(sem)
        if self.waited[eng].get(key, 0) < val:
            self.engs[eng].wait_ge(sem, val)
            self.waited[eng][key] = val

    def _deps(self, eng, reads, writes):
        deps = []
        for r in reads:
            lw = self.last_w.get(r)
            if lw:
                deps.append(lw)
        for w in writes:
            lw = self.last_w.get(w)
            if lw:
                deps.append(lw)
            deps.extend(self.readers.get(w, {}).values())
        for sem, val in deps:
            self._wait(eng, sem, val)

    def _commit(self, done, reads, writes):
        for r in reads:
            d = self.readers.setdefault(r, {})
            old = d.get(id(done[0]))
            if old is None or old[1] < done[1]:
                d[id(done[0])] = done
        for w in writes:
            self.last_w[w] = done
            self.readers[w] = {}

    def op(self, eng, fn, reads=(), writes=()):
        if self.dry:
            return
        ps = [r for r in reads if r.startswith("ps")]
        if ps:
            reads = [r for r in reads if not r.startswith("ps")]
            writes = list(writes) + ps
        self._deps(eng, reads, writes)
        inst = fn(self.engs[eng])
        self.cnt[eng] += 1
        inst.then_inc(self.sem[eng], 1)
        self._commit((self.sem[eng], self.cnt[eng]), reads, writes)

    def dma(self, q, out, in_, reads=(), writes=()):
        if self.dry:
            return
        self._deps(q, reads, writes)
        i = self.dcnt[q]
        self.dcnt[q] += 1
        sem = self.dsem[q][i % self.NS]
        val = 16 * (i // self.NS + 1)
        if val > 16:
            self._wait(q, sem, val - 16)
        self.engs[q].dma_start(out=out, in_=in_).then_inc(sem, 16)
        self._commit((sem, val), reads, writes)

    def barrier(self):
        if self.dry:
            return
        for eng in self.engs:
            for k in self.engs:
                if self.cnt[k] > 0:
                    self._wait(eng, self.sem[k], self.cnt[k])
            for q in ("sp", "pool"):
                for k in range(self.NS):
                    n = (self.dcnt[q] - k + self.NS - 1) // self.NS
                    if n > 0:
                        self._wait(eng, self.dsem[q][k], 16 * n)

    def finish(self):
        for q in ("sp", "pool"):
            for k in range(self.NS):
                n = (self.dcnt[q] - k + self.NS - 1) // self.NS
                if n > 0:
                    self._wait("sp", self.dsem[q][k], 16 * n)
        for k in self.engs:
            if k != "sp" and self.cnt[k] > 0:
                self._wait("sp", self.sem[k], self.cnt[k])


def psr(b, c0, c1):
    return [f"ps{b}"]


def build_program(cfg=None):
    cfg = cfg or {}
    nc = bass.Bass("TRN2", target_bir_lowering=False)

    def din(name, shape):
        return nc.dram_tensor(name, list(shape), F32, kind="ExternalInput").ap()

    def dout(name, shape):
        return nc.dram_tensor(name, list(shape), F32, kind="ExternalOutput").ap()

    xw = din("xw", [NTILES * 128, D])
    s_wkv = din("s_wkv", [NSQ, NH, HD, HD])
    s_shift = din("s_shift", [NSQ, RWKV_PROJ])
    s_conv = din("s_conv", [NSQ, 2, GR])
    w1g = din("w1g", [D, DFF]); w1u = din("w1u", [D, DFF]); w1d = din("w1d", [DFF, D])
    w2g = din("w2g", [D, DFF]); w2u = din("w2u", [D, DFF]); w2d = din("w2d", [DFF, D])
    w_in = din("w_in", [D, P_TOTAL]); w_out = din("w_out", [D, D])
    lw_w = din("lw_w", [96, GR]); lw_a = din("lw_a", [96, GR]); lw_g = din("lw_g", [256, GR])
    gvec = din("gvec", [128, 3 * KC])
    gfin = din("gfin", [128, D])
    mu_fm = din("mu_fm", [128, 28])
    pv = din("pv", [128, 8, 8])
    cw = din("cw", [128, 8, 3])
    cmat = din("cmat", [128, 2048])
    cmask_d = din("cmask", [128, NSQ * 128])
    rmask_d = din("rmask", [128, NSQ])

    WMETA = {"w1g": w1g, "w1u": w1u, "w1d": w1d, "w2g": w2g, "w2u": w2u, "w2d": w2d, "w_in": w_in, "w_out": w_out}
    WB16 = {k: nc.dram_tensor("b16_" + k, list(v.shape), BF16, kind="Internal").ap() for k, v in WMETA.items()}
    CROWS = 256
    wdone = {}

    y_out = dout("y_out", [NPOST * 128, D])
    wkv_p = dout("wkv_p", [NH, HD, HD])
    shift_p = dout("shift_p", [RWKV_PROJ])
    conv_p = dout("conv_p", [2, GR])
    wkv_s = dout("wkv_s", [NSQ, NH, HD, HD])
    shift_s = dout("shift_s", [NSQ, RWKV_PROJ])
    conv_s = dout("conv_s", [NSQ, 2, GR])

    with ExitStack() as es:
        es.enter_context(nc.allow_non_contiguous_dma(reason="small strided state rows"))

        def T(name, shape, dt=F32):
            return es.enter_context(nc.sbuf_tensor("sb_" + name, list(shape), dt))

        PS = [es.enter_context(nc.psum_tensor(f"psb{i}", [128, 512], F32)) for i in range(8)]
        pg = Prog(nc, es)

        xb = [T(f"xb{i}", [128, D]) for i in range(NT)]
        xn = T("xn", [128, D])
        hT = T("hT", [128, KC, N], BF16)
        act = T("act", [128, NFH, N], BF16)
        wsl = [T(f"wsl{i}", [128, KC, 256], BF16) for i in range(WSL)]
        dsl = [T(f"dsl{i}", [128, 1024], BF16) for i in range(DSL)]
        stat = T("stat", [128, 8])
        junk = act[:, 0:6, :].rearrange("p a b -> p (a b)")[:, 0:D]
        JUNK = [f"act{i}" for i in range(6)]
        cm = T("cm", [128, 2048])
        cmask = T("cmask", [128, NSQ, 128])
        rmask = T("rmask", [128, NSQ])
        gv = T("gv", [128, 3 * KC])
        gf = T("gf", [128, D])
        mu = T("mu", [128, 28])
        omu = T("omu", [128, 28])
        pvt = T("pvt", [128, 8, 8])
        cwt = T("cwt", [128, 8, 3])
        lww = T("lww", [96, GR], BF16)
        lwa = T("lwa", [96, GR], BF16)
        lwg = T("lwg", [128, 2, GR], BF16)
        ident = cm[:, 0:128]
        BO = cm[:, 128:256]
        BO64 = cm[:, 256:384]
        maskA = {1: cm[:, 384:640], NSQ: cm[:, 768:1024]}
        maskL = {1: cm[:, 640:768], NSQ: cm[:, 1024:1152]}
        scanm = {1: cm[:, 1152:1280], NSQ: cm[:, 1280:1408]}
        def TB(name, extra=0, dt=F32):
            return T(name, [128, N + extra], dt)
        pb = [TB(f"pb{i}", 1) for i in range(3)]
        qr, qk, qv = TB("qr"), TB("qk"), TB("qv")
        lgw, av, gg, kk, kp, ka, bon, t1, t2 = (TB(n) for n in ("lgw", "av", "gg", "kk", "kp", "ka", "bon", "t1", "t2"))
        twd = T("twd", [96, N], BF16); qad = T("qad", [96, N], BF16); sgd = T("sgd", [128, 2, N], BF16)
        carry = T("carry", [128, 28])
        prevS = T("prevS", [128, NSQ, LSQ])
        shT = T("shT", [128, 28, NSQ])
        shO = T("shO", [128, 28, NSQ])
        shP = T("shP", [128, 28])
        ub = T("ub", [128, N + 2])
        ucar = T("ucar", [128, 8, 2])
        uS = T("uS", [128, NSQ, LSQ + 2])
        cvt = t2
        arena = T("arena", [128, 8960])
        _ao = [0]

        def AV(words, dt=F32, shape3=None):
            a = _ao[0]
            _ao[0] += words
            assert _ao[0] <= 8960
            v = arena[:, a:a + words]
            if dt == BF16:
                v = v.bitcast(BF16)
            return v

        Lam, Lc, Le, G1, G2, G3, G4 = (AV(128) for _ in range(7))
        AR = AV(256).rearrange("p (a t) -> p a t", t=128)
        bt, kt, bh, kh = (AV(128) for _ in range(4))
        Vtm, Btm, Ktm = (AV(128) for _ in range(3))
        MB = [AV(256) for h in range(2)]
        MK = [AV(256) for h in range(2)]
        PQ = [AV(512).rearrange("p (a t) -> p a t", t=128) for i in range(2)]
        XX = [AV(128) for i in range(2)]
        Ysb, Yc, Ysq, Yr = (AV(128) for _ in range(4))
        E1 = AV(2048).rearrange("p (n t) -> p n t", t=128)
        E2 = AV(2048).rearrange("p (n t) -> p n t", t=128)
        _ao[0] = 0
        nLam, nLc, nLe, nG1, nG2, nG3, nG4, nbh, nkh = (AV(N) for _ in range(9))
        nAR = AV(NT * 128, BF16).rearrange("p (n a t) -> p n a t", a=2, t=128)
        nbt = AV(N // 2, BF16)
        nkt = AV(N // 2, BF16)
        nVBK = AV(NT * 3 * 64, BF16).rearrange("p (n a t) -> p n a t", a=3, t=128)
        nMB = AV(NT * 256, BF16).rearrange("p (n a t) -> p n a t", a=2, t=256)
        nMK = AV(NT * 256, BF16).rearrange("p (n a t) -> p n a t", a=2, t=256)
        nPQ = [AV(NT * 256, BF16).rearrange("p (n a t) -> p n a t", a=4, t=128) for i in range(2)]
        nXb = AV(NT * 128, BF16).rearrange("p (n a t) -> p n a t", a=2, t=128)
        nRH = AV(N // 2, BF16)
        nMS = AV(NT * 32, BF16).rearrange("p (n t) -> p n t", t=64)
        stg = [arena[0:32, 4096 + i * 128:4096 + (i + 1) * 128] for i in range(4)]
        ctmp = arena[:, 4608:4640]
        Sb = T("Sb", [128, 8, HD], BF16)
        identb = T("identb", [128, 128], BF16)
        I2 = T("I2", [128, 64])
        scanP = T("scanP", [128, N])
        Sp = T("Sp", [128, 8, HD])
        Ss = T("Ss", [128, NSQ, HD])
        Sld = E1[0:64]
        Sst = E2[0:64]

        wplan = {"w": [], "d": []}
        wissued = {"w": 0, "d": 0}
        wnext = {"w": 0, "d": 0}

        def issue_w(pool, idx):
            spec = wplan[pool][idx]
            name = spec[0]
            conv = wdone.get(name, False)
            src = WB16[name] if conv else WMETA[name]
            q = "sp" if conv else "pool"
            if pool == "w":
                s = idx % WSL
                _, c0, width = spec
                rds = [f"scr_{name}_{k}" for k in range(src.shape[0] // CROWS)] if conv else []
                pg.dma(q, wsl[s][:, :, 0:width],
                       src.rearrange("(kc p) f -> p kc f", p=128)[:, :, c0:c0 + width],
                       reads=rds, writes=[f"wsl{s}"])
            else:
                s = idx % DSL
                _, r0, c0 = spec
                rds = [f"scr_{name}_{r0 // CROWS}"] if conv else []
                pg.dma(q, dsl[s][:], src[r0:r0 + 128, c0:c0 + 1024], reads=rds, writes=[f"dsl{s}"])

        conv_plan = {0: [], 1: [], 2: []}
        if cfg.get("convert", True):
            for blk_, names in ((0, ("w1g", "w1u")), (1, ("w1d", "w2g")), (2, ("w2u", "w2d"))):
                for nm in names:
                    for k in range(WMETA[nm].shape[0] // CROWS):
                        conv_plan[blk_].append((nm, k))
        conv_pos = {}

        def convert_slot(blk, c):
            items = conv_plan.get(blk, [])
            pos = conv_pos.get(blk, 0)
            n = -(-(len(items) - pos) // (8 - c))
            for nm, k in items[pos:pos + n]:
                pg.dma("pool", WB16[nm][k * CROWS:(k + 1) * CROWS, :], WMETA[nm][k * CROWS:(k + 1) * CROWS, :],
                       writes=[f"scr_{nm}_{k}"])
            conv_pos[blk] = pos + n
            if pos + n >= len(items):
                for nm, _ in items:
                    wdone[nm] = True

        def wreq(pool, spec):
            idx = wnext[pool]
            wnext[pool] += 1
            if pg.dry:
                wplan[pool].append(spec)
            else:
                depth = PREFETCH_W if pool == "w" else PREFETCH_D
                while wissued[pool] < min(len(wplan[pool]), idx + 1 + depth):
                    issue_w(pool, wissued[pool])
                    wissued[pool] += 1
            if pool == "w":
                s = idx % WSL
                return wsl[s], f"wsl{s}"
            s = idx % DSL
            return dsl[s], f"dsl{s}"

        def proj_from(slot, res, sc0, width, bank, n=N):
            for kc in range(KC):
                pg.op("pe", lambda e, kc=kc: e.matmul(PS[bank][0:width, 0:n], lhsT=slot[:, kc, sc0:sc0 + width],
                                                        rhs=hT[:, kc, 0:n], start=(kc == 0), stop=(kc == KC - 1)),
                      reads=[res, "hT"], writes=psr(bank, 0, n))

        def proj(src, c0, width, bank, n=N):
            slot, res = wreq("w", (src, c0, width))
            proj_from(slot, res, 0, width, bank, n)

        def norm_T(gcol):
            for tt in range(NT):
                xr = f"xb{tt}"
                pg.op("act", lambda e, tt=tt: e.activation(out=junk[:], in_=xb[tt][:], func=AF.Square,
                                                            accum_out=stat[:, 0:1]),
                      reads=[xr], writes=JUNK + ["stat0"])
                pg.op("dve", lambda e: e.tensor_scalar(out=stat[:, 1:2], in0=stat[:, 0:1], scalar1=1.0 / D,
                                                       scalar2=RMS_EPS, op0=ALU.mult, op1=ALU.add),
                      reads=["stat0"], writes=["stat1"])
                pg.op("act", lambda e: e.activation(out=stat[:, 1:2], in_=stat[:, 1:2], func=AF.Sqrt),
                      reads=["stat1"], writes=["stat1"])
                pg.op("dve", lambda e: e.reciprocal(out=stat[:, 2:3], in_=stat[:, 1:2]),
                      reads=["stat1"], writes=["stat2"])
                pg.op("dve", lambda e, tt=tt: e.tensor_scalar(out=xn[:], in0=xb[tt][:], scalar1=stat[:, 2:3],
                                                              scalar2=None, op0=ALU.mult),
                      reads=[xr, "stat2"], writes=["xn"])
                for g4 in range(4):
                    if cfg.get("nt_stage", 3) < 2:
                        break
                    bank = g4 % 2
                    for j in range(4):
                        c = g4 * 4 + j
                        pg.op("pe", lambda e, c=c, j=j, bank=bank: e.transpose(
                            out=PS[bank][:, j * 128:(j + 1) * 128], in_=xn[:, c * 128:(c + 1) * 128], identity=ident),
                            reads=["xn", "cm"], writes=psr(bank, j * 128, (j + 1) * 128))
                    if cfg.get("nt_stage", 3) < 3:
                        continue
                    pg.op("dve", lambda e, g4=g4, bank=bank, tt=tt: e.tensor_tensor(
                        out=hT[:, g4 * 4:(g4 + 1) * 4, tt * 128:(tt + 1) * 128],
                        in0=PS[bank][:, 0:512].rearrange("p (c t) -> p c t", t=128),
                        in1=gv[:, gcol + g4 * 4:gcol + (g4 + 1) * 4].unsqueeze(2).to_broadcast([128, 4, 128]),
                        op=ALU.mult),
                        reads=psr(bank, 0, 512) + ["gv"], writes=["hT"])

        def down_proj(nchunk, src, row0, scale, tiles):
            for half in range(2):
                for c in range(nchunk):
                    slot, res = wreq("d", (src, row0 + c * 128, half * 1024))
                    for ti, tt in enumerate(tiles):
                        for bb in range(2):
                            bank = 2 + ti * 2 + bb
                            pg.op("pe", lambda e, c=c, tt=tt, bb=bb, bank=bank: e.matmul(
                                PS[bank][:, 0:512], lhsT=act[:, c, tt * 128:(tt + 1) * 128],
                                rhs=slot[:, bb * 512:(bb + 1) * 512], start=(c == 0), stop=(c == nchunk - 1)),
                                reads=[res, f"act{c}"], writes=psr(bank, 0, 512))
                for ti, tt in enumerate(tiles):
                    for bb in range(2):
                        bank = 2 + ti * 2 + bb
                        col = half * 1024 + bb * 512
                        pg.op("dve", lambda e, tt=tt, bank=bank, col=col: e.scalar_tensor_tensor(
                            out=xb[tt][:, col:col + 512], in0=PS[bank][:, 0:512], scalar=scale,
                            in1=xb[tt][:, col:col + 512], op0=ALU.mult, op1=ALU.add),
                            reads=psr(bank, 0, 512) + [f"xb{tt}"], writes=[f"xb{tt}"])

        def ffn(wg, wu, wd):
            for fh in range(FSPLIT):
                for fl in range(NFH):
                    f = fh * NFH + fl
                    gb = (fl % 2) * 2
                    if fl % 2 == 0:
                        gslot, gres = wreq("w", (wg, f * 128, 256))
                        uslot, ures = wreq("w", (wu, f * 128, 256))
                    proj_from(gslot, gres, (fl % 2) * 128, 128, gb)
                    proj_from(uslot, ures, (fl % 2) * 128, 128, gb + 1)
                    pg.op("act", lambda e, gb=gb: e.activation(out=t1[:], in_=PS[gb][:, 0:N], func=AF.Silu),
                          reads=psr(gb, 0, N), writes=["t1"])
                    pg.op("dve", lambda e, gb=gb, fl=fl: e.tensor_tensor(out=act[:, fl, :], in0=t1[:],
                                                                         in1=PS[gb + 1][:, 0:N], op=ALU.mult),
                          reads=["t1"] + psr(gb + 1, 0, N), writes=[f"act{fl}"])
                down_proj(NFH, wd, fh * NFH * 128, 0.5, list(range(NT)))

        def dv(fn, reads, writes, eng="dve"):
            pg.op(eng, fn, reads=reads, writes=writes)

        def chunk_id(kind, c=0):
            return {"r": c, "wd": 8, "k": 9 + c, "v": 17 + c, "ad": 25, "gd": 26 + c}[kind]

        CH_OFF = {}
        for c in range(8):
            CH_OFF[chunk_id("r", c)] = (OFF_R + c * 128, 128)
            CH_OFF[chunk_id("k", c)] = (OFF_K + c * 128, 128)
            CH_OFF[chunk_id("v", c)] = (OFF_V + c * 128, 128)
        CH_OFF[8] = (OFF_WD, 96)
        CH_OFF[25] = (OFF_AD, 96)
        CH_OFF[26] = (OFF_GD, 128)
        CH_OFF[27] = (OFF_GD + 128, 128)

        def lerp_chunk(cid, bank, pbuf, pres, q, qres, has_sample, last_prompt, npc):
            off, w = CH_OFF[cid]
            dv(lambda e: e.tensor_copy(out=pbuf[0:w, 0:1], in_=carry[0:w, cid:cid + 1]), ["carry"], [pres])
            dv(lambda e: e.activation(out=pbuf[0:w, 1:N + 1], in_=PS[bank][0:w, 0:N], func=AF.Copy),
               psr(bank, 0, N), [pres], eng="act")
            if npc > 0:
                dv(lambda e: e.tensor_copy(out=carry[0:w, cid:cid + 1], in_=pbuf[0:w, npc:npc + 1]), [pres], ["carry"])
            dv(lambda e: e.tensor_scalar(out=q[0:w, 0:N], in0=pbuf[0:w, 1:N + 1], scalar1=omu[0:w, cid:cid + 1],
                                         scalar2=None, op0=ALU.mult), [pres, "omu"], [qres])
            if npc > 0:
                dv(lambda e: e.scalar_tensor_tensor(out=q[0:w, 0:npc], in0=pbuf[0:w, 0:npc], scalar=mu[0:w, cid:cid + 1],
                                                    in1=q[0:w, 0:npc], op0=ALU.mult, op1=ALU.add),
                   [pres, "mu", qres], [qres])
            if last_prompt:
                dv(lambda e: e.tensor_copy(out=shP[0:w, cid:cid + 1], in_=pbuf[0:w, npc:npc + 1]), [pres], ["shP"],
                   eng="pool")
            if has_sample:
                s0 = npc + 1
                pS = pbuf[0:w, s0:s0 + 128].rearrange("p (n t) -> p n t", t=LSQ)
                dv(lambda e: e.tensor_copy(out=prevS[0:w, :, 1:LSQ], in_=pS[:, :, 0:LSQ - 1]), [pres], ["prevS"])
                dv(lambda e: e.tensor_copy(out=prevS[0:w, :, 0:1], in_=shT[0:w, cid, :].unsqueeze(2)), ["shT"], ["prevS"])
                dv(lambda e: e.tensor_copy(out=shO[0:w, cid, :].unsqueeze(2), in_=pS[:, :, LSQ - 1:LSQ]), [pres], ["shO"],
                   eng="pool")
                qS = q[0:w, npc:npc + 128].rearrange("p (n t) -> p n t", t=LSQ)
                dv(lambda e: e.scalar_tensor_tensor(out=qS, in0=prevS[0:w, :, :], scalar=mu[0:w, cid:cid + 1],
                                                    in1=qS, op0=ALU.mult, op1=ALU.add),
                   ["prevS", "mu", qres], [qres])

        def scan_tile(c, col0, nseq, is_last_prompt):
            L = 128 // nseq
            cs = slice(col0, col0 + 128)
            S = Sp[:, c:c + 1, :] if nseq == 1 else Ss[:, :, :]
            Sres = f"Sp{c}" if nseq == 1 else "Ss"
            nlev = 7 if nseq == 1 else 3
            dv(lambda e: e.tensor_tensor_scan(out=Lam[:], data0=scanm[nseq], data1=lgw[:, cs], initial=0.0,
                                              op0=ALU.mult, op1=ALU.add), ["cm", "lgw"], ["Lam"])
            Lv = Lam[:].rearrange("p (n t) -> p n t", t=L)
            dv(lambda e: e.tensor_tensor(out=Lc[:].rearrange("p (n t) -> p n t", t=L),
                                         in0=Lv[:, :, L - 1:L].to_broadcast([128, nseq, L]), in1=Lv, op=ALU.subtract),
               ["Lam"], ["Lc"])
            dv(lambda e: e.tensor_tensor(out=Le[:], in0=Lam[:], in1=lgw[:, cs], op=ALU.subtract), ["Lam", "lgw"], ["Le"],
               eng="pool")
            dv(lambda e: e.activation(out=G1[:], in_=Lam[:], func=AF.Exp), ["Lam"], ["G1"], eng="act")
            dv(lambda e: e.activation(out=G2[:], in_=Lam[:], func=AF.Exp, scale=-1.0), ["Lam"], ["G2"], eng="act")
            dv(lambda e: e.activation(out=G3[:], in_=Le[:], func=AF.Exp), ["Le"], ["G3"], eng="act")
            dv(lambda e: e.activation(out=G4[:], in_=Lc[:], func=AF.Exp), ["Lc"], ["G4"], eng="act")
            dv(lambda e: e.scalar_tensor_tensor(out=AR[:, 0, :], in0=kk[:, cs], scalar=-1.0, in1=G3[:],
                                                op0=ALU.mult, op1=ALU.mult), ["kk", "G3"], ["AR0"])
            dv(lambda e: e.tensor_tensor(out=AR[:, 1, :], in0=qr[:, cs], in1=G1[:], op=ALU.mult), ["qr", "G1"], ["AR1"])
            dv(lambda e: e.tensor_tensor(out=bt[:], in0=ka[:, cs], in1=G2[:], op=ALU.mult), ["ka", "G2"], ["bt"])
            dv(lambda e: e.tensor_tensor(out=kt[:], in0=kp[:, cs], in1=G2[:], op=ALU.mult), ["kp", "G2"], ["kt"])
            dv(lambda e: e.tensor_tensor(out=bh[:], in0=ka[:, cs], in1=G4[:], op=ALU.mult), ["ka", "G4"], ["bh"], eng=cfg.get("bh_eng", "pool"))
            dv(lambda e: e.tensor_tensor(out=kh[:], in0=kp[:, cs], in1=G4[:], op=ALU.mult), ["kp", "G4"], ["kh"], eng=cfg.get("bh_eng", "pool"))
            if cfg.get('sc_stage', 9) <= 1:
                return
            for j, (src, sres) in enumerate(((qv[:, cs], "qv"), (bh[:], "bh"), (kh[:], "kh"))):
                if j >= cfg.get("ntr", 3):
                    break
                pg.op("pe", lambda e, j=j, src=src: e.transpose(out=PS[0][:, j * 128:(j + 1) * 128], in_=src, identity=ident),
                      reads=[sres, "cm"], writes=psr(0, j * 128, (j + 1) * 128))
            dv(lambda e: e.activation(out=Vtm[:], in_=PS[0][:, 0:128], func=AF.Copy), psr(0, 0, 128), ["Vtm"], eng="act")
            if nseq == 1 and cfg.get("ntr", 3) == 3:
                dv(lambda e: e.tensor_copy(out=Btm[:], in_=PS[0][:, 128:256]), psr(0, 128, 256), ["Btm"])
                dv(lambda e: e.activation(out=Ktm[:], in_=PS[0][:, 256:384], func=AF.Copy), psr(0, 256, 384), ["Ktm"], eng="act")
            if cfg.get('sc_sub', 9) <= 1:
                return
            for h in range(2):
                hs = slice(h * 64, (h + 1) * 64)
                pg.op("pe", lambda e, hs=hs, h=h: e.matmul(PS[1][:, h * 256:(h + 1) * 256], lhsT=bt[hs, :],
                                                            rhs=AR[hs, :, :].rearrange("p a t -> p (a t)"), start=True, stop=True),
                      reads=["bt", "AR0", "AR1"], writes=psr(1, h * 256, (h + 1) * 256))
                pg.op("pe", lambda e, hs=hs, h=h: e.matmul(PS[2][:, h * 256:(h + 1) * 256], lhsT=kt[hs, :],
                                                            rhs=AR[hs, :, :].rearrange("p a t -> p (a t)"), start=True, stop=True),
                      reads=["kt", "AR0", "AR1"], writes=psr(2, h * 256, (h + 1) * 256))
                pg.op("pe", lambda e, hs=hs, h=h: e.matmul(PS[3][:, h * 128:(h + 1) * 128], lhsT=AR[hs, 0, :],
                                                            rhs=bt[hs, :], start=True, stop=True),
                      reads=["bt", "AR0"], writes=psr(3, h * 128, (h + 1) * 128))
            if cfg.get('sc_sub', 9) <= 2:
                return
            for h in range(2):
                dv(lambda e, h=h: e.tensor_tensor(out=MB[h][:], in0=PS[1][:, h * 256:(h + 1) * 256], in1=maskA[nseq], op=ALU.mult),
                   psr(1, h * 256, (h + 1) * 256) + ["cm"], [f"MB{h}"])
                dv(lambda e, h=h: e.tensor_tensor(out=MK[h][:], in0=PS[2][:, h * 256:(h + 1) * 256], in1=maskA[nseq], op=ALU.mult),
                   psr(2, h * 256, (h + 1) * 256) + ["cm"], [f"MK{h}"])
                dv(lambda e, h=h: e.tensor_tensor(out=PQ[0][:, h, :], in0=PS[3][:, h * 128:(h + 1) * 128], in1=maskL[nseq], op=ALU.mult),
                   psr(3, h * 128, (h + 1) * 128) + ["cm"], ["PQ0"])
                dv(lambda e, h=h: e.tensor_copy(out=PQ[0][:, 2 + h, :], in_=MB[h][:, 0:128]), [f"MB{h}"], ["PQ0"], eng="pool")
            if cfg.get('sc_stage', 9) <= 2:
                return
            if nseq > 1:
                dv(lambda e: e.tensor_tensor(out=E1[:], in0=AR[:, 0:1, :].to_broadcast([128, nseq, 128]), in1=cmask[:], op=ALU.mult),
                   ["AR0", "cmask"], ["E1"])
                dv(lambda e: e.tensor_tensor(out=E2[:], in0=AR[:, 1:2, :].to_broadcast([128, nseq, 128]), in1=cmask[:], op=ALU.mult),
                   ["AR1", "cmask"], ["E2"], eng="pool")
            for h in range(2):
                hs = slice(h * 64, (h + 1) * 64)
                for n in range(nseq):
                    lhs = AR[hs, 0, :] if nseq == 1 else E1[hs, n, :]
                    pg.op("pe", lambda e, hs=hs, h=h, n=n, lhs=lhs: e.matmul(PS[4][:, h * 64:(h + 1) * 64], lhsT=lhs,
                                                                                rhs=S[hs, n, :], start=(n == 0), stop=False),
                          reads=["AR0", "E1", Sres], writes=psr(4, 0, 128))
                pg.op("pe", lambda e, h=h: e.matmul(PS[4][:, h * 64:(h + 1) * 64], lhsT=MK[h][:, 0:128],
                                                     rhs=Vtm[:, h * 64:(h + 1) * 64], start=False, stop=True),
                      reads=[f"MK{h}", "Vtm"], writes=psr(4, 0, 128))
            dv(lambda e: e.tensor_copy(out=XX[0][:], in_=PS[4][:, 0:128]), psr(4, 0, 128), ["XX0"])
            if cfg.get('sc_stage', 9) <= 3:
                return
            cur = 0
            for lev in range(nlev):
                pq = PQ[lev % 2]
                pqr = f"PQ{lev % 2}"
                for h in range(2):
                    pg.op("pe", lambda e, h=h, pq=pq, cur=cur: e.matmul(PS[4][:, 128 + h * 64:128 + (h + 1) * 64], lhsT=pq[:, 2 + h, :],
                                                                          rhs=XX[cur][:, h * 64:(h + 1) * 64], start=True, stop=True),
                          reads=[pqr, f"XX{cur}"], writes=psr(4, 128, 256))
                dv(lambda e, cur=cur: e.tensor_tensor(out=XX[1 - cur][:], in0=XX[cur][:], in1=PS[4][:, 128:256], op=ALU.add),
                   [f"XX{cur}"] + psr(4, 128, 256), [f"XX{1 - cur}"])
                cur = 1 - cur
                if lev < nlev - 1:
                    nq = PQ[(lev + 1) % 2]
                    nqr = f"PQ{(lev + 1) % 2}"
                    for h in range(2):
                        pg.op("pe", lambda e, h=h, pq=pq: e.matmul(PS[5][:, h * 128:(h + 1) * 128], lhsT=pq[:, 2 + h, :],
                                                                    rhs=pq[:, h, :], start=True, stop=True),
                              reads=[pqr], writes=psr(5, h * 128, (h + 1) * 128))
                        pg.op("pe", lambda e, h=h, pq=pq: e.matmul(PS[5][:, (2 + h) * 128:(3 + h) * 128], lhsT=pq[:, h, :],
                                                                    rhs=pq[:, 2 + h, :], start=True, stop=True),
                              reads=[pqr], writes=psr(5, (2 + h) * 128, (3 + h) * 128))
                    dv(lambda e, nq=nq: e.activation(out=nq[:].rearrange("p a t -> p (a t)"), in_=PS[5][:, 0:512], func=AF.Copy),
                       psr(5, 0, 512), [nqr], eng="act")
            U = XX[cur]
            Ures = f"XX{cur}"
            if cfg.get('sc_stage', 9) <= 4:
                return
            for h in range(2):
                hs = slice(h * 64, (h + 1) * 64)
                for n in range(nseq):
                    rhs = AR[hs, 1, :] if nseq == 1 else E2[hs, n, :]
                    pg.op("pe", lambda e, hs=hs, n=n, rhs=rhs: e.matmul(PS[6][hs, 0:128], lhsT=S[hs, n, :], rhs=rhs,
                                                                          start=(n == 0), stop=False),
                          reads=["AR1", "E2", Sres], writes=psr(6, 0, 128))
                pg.op("pe", lambda e, hs=hs, h=h: e.matmul(PS[6][hs, 0:128], lhsT=U[:, h * 64:(h + 1) * 64], rhs=MB[h][:, 128:256],
                                                            start=False, stop=False),
                      reads=[Ures, f"MB{h}"], writes=psr(6, 0, 128))
                pg.op("pe", lambda e, hs=hs, h=h: e.matmul(PS[6][hs, 0:128], lhsT=Vtm[:, h * 64:(h + 1) * 64], rhs=MK[h][:, 128:256],
                                                            start=False, stop=True),
                      reads=["Vtm", f"MK{h}"], writes=psr(6, 0, 128))
            dv(lambda e: e.activation(out=Ysb[:], in_=PS[6][:, 0:128], func=AF.Copy), psr(6, 0, 128), ["Ysb"], eng="act")
            if cfg.get('sc_stage', 9) <= 5:
                return
            if nseq > 1:
                dv(lambda e: e.tensor_tensor(out=E1[:], in0=PS[0][:, 128:256].unsqueeze(1).to_broadcast([128, nseq, 128]),
                                             in1=rmask[:].unsqueeze(2).to_broadcast([128, nseq, 128]), op=ALU.mult),
                   psr(0, 128, 256) + ["rmask"], ["E1"])
                dv(lambda e: e.tensor_tensor(out=E2[:], in0=PS[0][:, 256:384].unsqueeze(1).to_broadcast([128, nseq, 128]),
                                             in1=rmask[:].unsqueeze(2).to_broadcast([128, nseq, 128]), op=ALU.mult),
                   psr(0, 256, 384) + ["rmask"], ["E2"])
            G1v = G1[:].rearrange("p (n t) -> p n t", t=L)
            dv(lambda e: e.tensor_tensor(out=S, in0=S, in1=G1v[:, :, L - 1:L].to_broadcast([128, nseq, HD]), op=ALU.mult),
               [Sres, "G1"], [Sres])
            for n0 in range(0, nseq, 4):
                g = min(4, nseq - n0)
                for n in range(n0, n0 + g):
                    sl = n - n0
                    lb = Btm[:] if nseq == 1 else E1[:, n, :]
                    lk = Ktm[:] if nseq == 1 else E2[:, n, :]
                    pg.op("pe", lambda e, sl=sl, lb=lb: e.matmul(PS[7][:, sl * 128:(sl + 1) * 128], lhsT=lb, rhs=U[:], start=True, stop=False),
                          reads=["Btm", "E1", Ures], writes=psr(7, 0, 512))
                    pg.op("pe", lambda e, sl=sl, lk=lk: e.matmul(PS[7][:, sl * 128:(sl + 1) * 128], lhsT=lk, rhs=Vtm[:], start=False, stop=True),
                          reads=["Ktm", "E2", "Vtm"], writes=psr(7, 0, 512))
                for h in range(2):
                    hs = slice(h * 64, (h + 1) * 64)
                    dv(lambda e, hs=hs, h=h, n0=n0, g=g: e.tensor_tensor(
                        out=S[hs, n0:n0 + g, :], in0=S[hs, n0:n0 + g, :],
                        in1=PS[7][hs, 0:g * 128].rearrange("p (s c) -> p s c", c=128)[:, :, h * 64:(h + 1) * 64], op=ALU.add),
                       [Sres, "ps7"], [Sres])
            if cfg.get('sc_stage', 9) <= 6:
                return
            pg.op("pe", lambda e: e.matmul(PS[6][:, 128:256], lhsT=BO64, rhs=Ysb[:], start=True, stop=True),
                  reads=["cm", "Ysb"], writes=psr(6, 128, 256))
            dv(lambda e: e.tensor_tensor(out=Yc[:], in0=Ysb[:], in1=PS[6][:, 128:256], op=ALU.subtract),
               ["Ysb"] + psr(6, 128, 256), ["Yc"])
            dv(lambda e: e.tensor_tensor(out=Ysq[:], in0=Yc[:], in1=Yc[:], op=ALU.mult), ["Yc"], ["Ysq"], eng="pool")
            pg.op("pe", lambda e: e.matmul(PS[6][:, 256:384], lhsT=BO64, rhs=Ysq[:], start=True, stop=True),
                  reads=["cm", "Ysq"], writes=psr(6, 256, 384))
            dv(lambda e: e.tensor_scalar(out=Yr[:], in0=PS[6][:, 256:384], scalar1=GN_EPS, scalar2=None, op0=ALU.add),
               psr(6, 256, 384), ["Yr"])
            dv(lambda e: e.activation(out=Yr[:], in_=Yr[:], func=AF.Sqrt), ["Yr"], ["Yr"], eng="act")
            dv(lambda e: e.reciprocal(out=Yr[:], in_=Yr[:]), ["Yr"], ["Yr"])
            dv(lambda e: e.tensor_tensor(out=Yc[:], in0=Yc[:], in1=Yr[:], op=ALU.mult), ["Yc", "Yr"], ["Yc"])
            dv(lambda e: e.tensor_scalar(out=Yc[:], in0=Yc[:], scalar1=pvt[:, c, 5:6], scalar2=pvt[:, c, 6:7],
                                         op0=ALU.mult, op1=ALU.add), ["Yc", "pvt"], ["Yc"])
            dv(lambda e: e.tensor_tensor(out=Yc[:], in0=Yc[:], in1=bon[:, cs], op=ALU.add), ["Yc", "bon"], ["Yc"])
            dv(lambda e: e.tensor_tensor(out=act[:, c, cs], in0=Yc[:], in1=gg[:, cs], op=ALU.mult), ["Yc", "gg"], [f"act{c}"])

        def scan_block(c, ntp, last_ti, need_out=True):
            W = ntp * 128
            v3 = lambda ap: ap[:, 0:W].rearrange("p (n t) -> p n t", t=128)
            Xreg = lambda ti, h: (4 + ti // 2, (ti % 2) * 256 + h * 128)
            sqbank = [3, 6, 7]
            dv(lambda e: e.tensor_tensor_scan(out=nLam[:, 0:W], data0=scanP[:, 0:W], data1=lgw[:, 0:W], initial=0.0,
                                              op0=ALU.mult, op1=ALU.add), ["scanP", "lgw"], ["nLam"])
            dv(lambda e: e.tensor_tensor(out=v3(nLc), in0=v3(nLam)[:, :, 127:128].to_broadcast([128, ntp, 128]), in1=v3(nLam),
                                         op=ALU.subtract), ["nLam"], ["nLc"])
            dv(lambda e: e.tensor_tensor(out=nLe[:, 0:W], in0=nLam[:, 0:W], in1=lgw[:, 0:W], op=ALU.subtract), ["nLam", "lgw"], ["nLe"],
               eng="pool")
            dv(lambda e: e.activation(out=nG1[:, 0:W], in_=nLam[:, 0:W], func=AF.Exp), ["nLam"], ["nG1"], eng="act")
            dv(lambda e: e.activation(out=nG2[:, 0:W], in_=nLam[:, 0:W], func=AF.Exp, scale=-1.0), ["nLam"], ["nG2"], eng="act")
            dv(lambda e: e.activation(out=nG3[:, 0:W], in_=nLe[:, 0:W], func=AF.Exp), ["nLe"], ["nG3"], eng="act")
            dv(lambda e: e.activation(out=nG4[:, 0:W], in_=nLc[:, 0:W], func=AF.Exp), ["nLc"], ["nG4"], eng="act")
            dv(lambda e: e.scalar_tensor_tensor(out=nAR[:, 0:ntp, 0, :], in0=v3(kk), scalar=-1.0, in1=v3(nG3),
                                                op0=ALU.mult, op1=ALU.mult), ["kk", "nG3"], ["nAR"])
            dv(lambda e: e.tensor_tensor(out=nAR[:, 0:ntp, 1, :], in0=v3(qr), in1=v3(nG1), op=ALU.mult), ["qr", "nG1"], ["nAR"])
            dv(lambda e: e.tensor_tensor(out=nbt[:, 0:W], in0=ka[:, 0:W], in1=nG2[:, 0:W], op=ALU.mult), ["ka", "nG2"], ["nbt"])
            dv(lambda e: e.tensor_tensor(out=nkt[:, 0:W], in0=kp[:, 0:W], in1=nG2[:, 0:W], op=ALU.mult), ["kp", "nG2"], ["nkt"])
            dv(lambda e: e.tensor_tensor(out=nbh[:, 0:W], in0=ka[:, 0:W], in1=nG4[:, 0:W], op=ALU.mult), ["ka", "nG4"], ["nbh"], eng="pool")
            dv(lambda e: e.tensor_tensor(out=nkh[:, 0:W], in0=kp[:, 0:W], in1=nG4[:, 0:W], op=ALU.mult), ["kp", "nG4"], ["nkh"], eng="pool")
            for ti in range(ntp):
                cs = slice(ti * 128, (ti + 1) * 128)
                for j, (src, sres) in enumerate(((qv[:, cs], "qv"), (nbh[:, cs], "nbh"), (nkh[:, cs], "nkh"))):
                    pg.op("pe", lambda e, j=j, src=src: e.transpose(out=PS[0][:, j * 128:(j + 1) * 128], in_=src, identity=ident),
                          reads=[sres, "cm"], writes=["ps0"])
                dv(lambda e, ti=ti: e.activation(out=nVBK[:, ti, :, :], in_=PS[0][:, 0:384].rearrange("p (a t) -> p a t", t=128),
                                                 func=AF.Copy), ["ps0"], ["nVBK"], eng="act")
                for h in range(2):
                    hs = slice(h * 64, (h + 1) * 64)
                    pg.op("pe", lambda e, hs=hs, h=h, ti=ti, cs=cs: e.matmul(PS[1][:, h * 256:(h + 1) * 256], lhsT=nbt[hs, cs],
                                                                                rhs=nAR[hs, ti, :, :].rearrange("p a t -> p (a t)"), start=True, stop=True),
                          reads=["nbt", "nAR"], writes=["ps1"])
                    pg.op("pe", lambda e, hs=hs, h=h, ti=ti, cs=cs: e.matmul(PS[2][:, h * 256:(h + 1) * 256], lhsT=nkt[hs, cs],
                                                                                rhs=nAR[hs, ti, :, :].rearrange("p a t -> p (a t)"), start=True, stop=True),
                          reads=["nkt", "nAR"], writes=["ps2"])
                    pg.op("pe", lambda e, hs=hs, h=h, ti=ti, cs=cs: e.matmul(PS[3][:, h * 128:(h + 1) * 128], lhsT=nAR[hs, ti, 0, :],
                                                                                rhs=nbt[hs, cs], start=True, stop=True),
                          reads=["nbt", "nAR"], writes=["ps3"])
                dv(lambda e, ti=ti: e.tensor_tensor(out=nMB[:, ti, :, :], in0=PS[1][:, 0:512].rearrange("p (a t) -> p a t", t=256),
                                                    in1=maskA[1].unsqueeze(1).to_broadcast([128, 2, 256]), op=ALU.mult),
                   ["ps1", "cm"], ["nMB"])
                dv(lambda e, ti=ti: e.tensor_tensor(out=nMK[:, ti, :, :], in0=PS[2][:, 0:512].rearrange("p (a t) -> p a t", t=256),
                                                    in1=maskA[1].unsqueeze(1).to_broadcast([128, 2, 256]), op=ALU.mult),
                   ["ps2", "cm"], ["nMK"])
                dv(lambda e, ti=ti: e.tensor_tensor(out=nPQ[0][:, ti, 0:2, :], in0=PS[3][:, 0:256].rearrange("p (a t) -> p a t", t=128),
                                                    in1=maskL[1].unsqueeze(1).to_broadcast([128, 2, 128]), op=ALU.mult),
                   ["ps3", "cm"], ["nPQ0"])
                dv(lambda e, ti=ti: e.tensor_copy(out=nPQ[0][:, ti, 2:4, :], in_=nMB[:, ti, :, 0:128]), ["nMB"], ["nPQ0"], eng="pool")
            dv(lambda e: e.memset(PS[4][:, 0:512], 0.0), [], ["ps4"])
            if ntp > 2:
                dv(lambda e: e.memset(PS[5][:, 0:256], 0.0), [], ["ps5"])
            for ti in range(ntp):
                for h in range(2):
                    hs = slice(h * 64, (h + 1) * 64)
                    xb_, xc = Xreg(ti, h)
                    pg.op("pe", lambda e, hs=hs, h=h, ti=ti, xb_=xb_, xc=xc: e.matmul(PS[xb_][:, xc:xc + 64], lhsT=nAR[hs, ti, 0, :],
                                                                                        rhs=identb[hs, h * 64:(h + 1) * 64], start=False, stop=True),
                          reads=["nAR", "identb"], writes=[f"ps{xb_}"])
                    pg.op("pe", lambda e, h=h, ti=ti, xb_=xb_, xc=xc: e.matmul(PS[xb_][:, xc + 64:xc + 128], lhsT=nMK[:, ti, h, 0:128],
                                                                                 rhs=nVBK[:, ti, 0, h * 64:(h + 1) * 64], start=False, stop=True),
                          reads=["nMK", "nVBK"], writes=[f"ps{xb_}"])

            def copy_X():
                dv(lambda e: e.activation(out=nXb[:, 0:min(ntp, 2), :, :].rearrange("p n a t -> p (n a t)"),
                                          in_=PS[4][:, 0:min(ntp, 2) * 256], func=AF.Copy), ["ps4"], ["nXb"], eng="act")
                if ntp > 2:
                    dv(lambda e: e.tensor_copy(out=nXb[:, 2, :, :].rearrange("p a t -> p (a t)"), in_=PS[5][:, 0:256]), ["ps5"], ["nXb"])

            for lev in range(7):
                pq = nPQ[lev % 2]
                pqr = f"nPQ{lev % 2}"
                copy_X()
                for ti in range(ntp):
                    for h in range(2):
                        xb_, xc = Xreg(ti, h)
                        pg.op("pe", lambda e, h=h, ti=ti, xb_=xb_, xc=xc, pq=pq: e.matmul(PS[xb_][:, xc:xc + 128], lhsT=pq[:, ti, 2 + h, :],
                                                                                           rhs=nXb[:, ti, h, :], start=False, stop=True),
                              reads=[pqr, "nXb"], writes=[f"ps{xb_}"])
                if lev < 6:
                    nq = nPQ[(lev + 1) % 2]
                    nqr = f"nPQ{(lev + 1) % 2}"
                    for ti in range(ntp):
                        sb_ = sqbank[ti]
                        for h in range(2):
                            pg.op("pe", lambda e, h=h, ti=ti, sb_=sb_, pq=pq: e.matmul(PS[sb_][:, h * 128:(h + 1) * 128], lhsT=pq[:, ti, 2 + h, :],
                                                                                        rhs=pq[:, ti, h, :], start=True, stop=True),
                                  reads=[pqr], writes=[f"ps{sb_}"])
                            pg.op("pe", lambda e, h=h, ti=ti, sb_=sb_, pq=pq: e.matmul(PS[sb_][:, (2 + h) * 128:(3 + h) * 128], lhsT=pq[:, ti, h, :],
                                                                                        rhs=pq[:, ti, 2 + h, :], start=True, stop=True),
                                  reads=[pqr], writes=[f"ps{sb_}"])
                        if ti % 2 == 0:
                            dv(lambda e, ti=ti, sb_=sb_, nq=nq: e.tensor_copy(out=nq[:, ti, :, :].rearrange("p a t -> p (a t)"), in_=PS[sb_][:, 0:512]),
                               [f"ps{sb_}"], [nqr])
                        else:
                            dv(lambda e, ti=ti, sb_=sb_, nq=nq: e.activation(out=nq[:, ti, :, :].rearrange("p a t -> p (a t)"), in_=PS[sb_][:, 0:512],
                                                                              func=AF.Copy), [f"ps{sb_}"], [nqr], eng="act")
            copy_X()
            if need_out:
                dv(lambda e: e.memset(PS[7][:, 0:W], 0.0), [], ["ps7"])
            for ti in range(ntp):
                cs = slice(ti * 128, (ti + 1) * 128)
                for h in range(2):
                    hs = slice(h * 64, (h + 1) * 64)
                    if need_out:
                        pg.op("pe", lambda e, hs=hs, h=h, ti=ti, cs=cs: e.matmul(PS[7][hs, cs], lhsT=nXb[:, ti, h, 0:64], rhs=nMB[:, ti, h, 128:256],
                                                                                    start=False, stop=False), reads=["nXb", "nMB"], writes=["ps7"])
                        pg.op("pe", lambda e, hs=hs, h=h, ti=ti, cs=cs: e.matmul(PS[7][hs, cs], lhsT=identb[hs, h * 64:(h + 1) * 64], rhs=nAR[hs, ti, 1, :],
                                                                                    start=False, stop=True), reads=["identb", "nAR"], writes=["ps7"])
                    pg.op("pe", lambda e, hs=hs, h=h, ti=ti: e.matmul(PS[3][hs, ti * 64:(ti + 1) * 64], lhsT=nXb[:, ti, h, 0:64],
                                                                       rhs=nVBK[:, ti, 1, h * 64:(h + 1) * 64], start=True, stop=True),
                          reads=["nXb", "nVBK"], writes=["ps3"])
            if need_out:
                dv(lambda e: e.activation(out=nRH[:, 0:W], in_=PS[7][:, 0:W], func=AF.Copy), ["ps7"], ["nRH"], eng="act")
            for ti in range(ntp):
                gc = ti * 128 + 127
                dv(lambda e, ti=ti, gc=gc: e.scalar_tensor_tensor(out=nMS[:, ti, :], in0=I2[:], scalar=nG1[:, gc:gc + 1],
                                                                  in1=PS[3][:, ti * 64:(ti + 1) * 64], op0=ALU.mult, op1=ALU.add),
                   ["I2", "nG1", "ps3"], ["nMS"])
            dv(lambda e: e.memset(PS[6][:, 0:ntp * 64], 0.0), [], ["ps6"])
            if need_out:
                dv(lambda e: e.memset(PS[0][:, 0:W], 0.0), [], ["ps0"])
            for ti in range(ntp):
                cs = slice(ti * 128, (ti + 1) * 128)
                for h in range(2):
                    hs = slice(h * 64, (h + 1) * 64)
                    hc = slice(h * 64, (h + 1) * 64)
                    pg.op("pe", lambda e, hs=hs, hc=hc, h=h, ti=ti: e.matmul(PS[6][hs, ti * 64:(ti + 1) * 64], lhsT=nVBK[:, ti, 1, hc],
                                                                                rhs=nXb[:, ti, h, 64:128], start=False, stop=False),
                          reads=["nVBK", "nXb"], writes=["ps6"])
                    pg.op("pe", lambda e, hs=hs, hc=hc, h=h, ti=ti: e.matmul(PS[6][hs, ti * 64:(ti + 1) * 64], lhsT=nVBK[:, ti, 2, hc],
                                                                                rhs=nVBK[:, ti, 0, hc], start=False, stop=False),
                          reads=["nVBK"], writes=["ps6"])
                    if need_out:
                        pg.op("pe", lambda e, hs=hs, hc=hc, h=h, ti=ti, cs=cs: e.matmul(PS[0][hs, cs], lhsT=nXb[:, ti, h, 64:128],
                                                                                           rhs=nMB[:, ti, h, 128:256], start=False, stop=False),
                              reads=["nXb", "nMB"], writes=["ps0"])
                        pg.op("pe", lambda e, hs=hs, hc=hc, h=h, ti=ti, cs=cs: e.matmul(PS[0][hs, cs], lhsT=nVBK[:, ti, 0, hc],
                                                                                           rhs=nMK[:, ti, h, 128:256], start=False, stop=False),
                              reads=["nVBK", "nMK"], writes=["ps0"])
            for ti in range(ntp):
                cs = slice(ti * 128, (ti + 1) * 128)
                for h in range(2):
                    if not need_out:
                        break
                    hs = slice(h * 64, (h + 1) * 64)
                    pg.op("pe", lambda e, hs=hs, cs=cs: e.matmul(PS[0][hs, cs], lhsT=Sb[hs, c, :], rhs=nRH[hs, cs], start=False, stop=True),
                          reads=[f"Sb{c}", "nRH"], writes=["ps0"])
                for h in range(2):
                    hs = slice(h * 64, (h + 1) * 64)
                    pg.op("pe", lambda e, hs=hs, ti=ti: e.matmul(PS[6][hs, ti * 64:(ti + 1) * 64], lhsT=nMS[hs, ti, :], rhs=Sb[hs, c, :],
                                                                  start=False, stop=True),
                          reads=[f"Sb{c}", "nMS"], writes=["ps6"])
                dv(lambda e, ti=ti: e.activation(out=Sb[:, c, :], in_=PS[6][:, ti * 64:(ti + 1) * 64], func=AF.Copy), ["ps6"], [f"Sb{c}"],
                   eng="act")
                if ti == last_ti:
                    dv(lambda e, ti=ti: e.tensor_copy(out=Sp[:, c, :], in_=PS[6][:, ti * 64:(ti + 1) * 64]), ["ps6"], [f"Sp{c}"])
            if not need_out:
                return
            Yb, Ycb, Yqb, Yrb = nLc, nLe, nG2, nG3
            dv(lambda e: e.activation(out=Yb[:, 0:W], in_=PS[0][:, 0:W], func=AF.Copy), ["ps0"], ["nLc"], eng="act")
            pg.op("pe", lambda e: e.matmul(PS[1][:, 0:W], lhsT=BO64, rhs=Yb[:, 0:W], start=True, stop=True), reads=["cm", "nLc"], writes=["ps1"])
            dv(lambda e: e.tensor_tensor(out=Ycb[:, 0:W], in0=Yb[:, 0:W], in1=PS[1][:, 0:W], op=ALU.subtract), ["nLc", "ps1"], ["nLe"])
            dv(lambda e: e.tensor_tensor(out=Yqb[:, 0:W], in0=Ycb[:, 0:W], in1=Ycb[:, 0:W], op=ALU.mult), ["nLe"], ["nG2"], eng="pool")
            pg.op("pe", lambda e: e.matmul(PS[2][:, 0:W], lhsT=BO64, rhs=Yqb[:, 0:W], start=True, stop=True), reads=["cm", "nG2"], writes=["ps2"])
            dv(lambda e: e.tensor_scalar(out=Yrb[:, 0:W], in0=PS[2][:, 0:W], scalar1=GN_EPS, scalar2=None, op0=ALU.add), ["ps2"], ["nG3"])
            dv(lambda e: e.activation(out=Yrb[:, 0:W], in_=Yrb[:, 0:W], func=AF.Sqrt), ["nG3"], ["nG3"], eng="act")
            dv(lambda e: e.reciprocal(out=Yrb[:, 0:W], in_=Yrb[:, 0:W]), ["nG3"], ["nG3"])
            dv(lambda e: e.tensor_tensor(out=Ycb[:, 0:W], in0=Ycb[:, 0:W], in1=Yrb[:, 0:W], op=ALU.mult), ["nLe", "nG3"], ["nLe"])
            dv(lambda e: e.tensor_scalar(out=Ycb[:, 0:W], in0=Ycb[:, 0:W], scalar1=pvt[:, c, 5:6], scalar2=pvt[:, c, 6:7],
                                         op0=ALU.mult, op1=ALU.add), ["nLe", "pvt"], ["nLe"])
            dv(lambda e: e.tensor_tensor(out=Ycb[:, 0:W], in0=Ycb[:, 0:W], in1=bon[:, 0:W], op=ALU.add), ["nLe", "bon"], ["nLe"])
            dv(lambda e: e.tensor_tensor(out=act[:, c, 0:W], in0=Ycb[:, 0:W], in1=gg[:, 0:W], op=ALU.mult), ["nLe", "gg"], [f"act{c}"])

        def state_out(c, nseq, dst):
            S = Sp[:, c:c + 1, :] if nseq == 1 else Ss[:, :, :]
            Sres = f"Sp{c}" if nseq == 1 else "Ss"
            for n0 in range(0, nseq, 4):
                g = min(4, nseq - n0)
                bank = 6 + (n0 // 4) % 2
                for n in range(n0, n0 + g):
                    sl = n - n0
                    pg.op("pe", lambda e, n=n, sl=sl, bank=bank: e.transpose(out=PS[bank][0:64, sl * 128:(sl + 1) * 128], in_=S[:, n, :],
                                                                             identity=ident),
                          reads=[Sres, "cm"], writes=psr(bank, 0, 512))
                eng = "dve" if (n0 // 4) % 2 == 0 else "act"
                if eng == "dve":
                    dv(lambda e, n0=n0, g=g, bank=bank: e.tensor_copy(out=Sst[:, n0:n0 + g, :].rearrange("p n t -> p (n t)"),
                                                                     in_=PS[bank][0:64, 0:g * 128]), psr(bank, 0, 512), ["E2"])
                else:
                    dv(lambda e, n0=n0, g=g, bank=bank: e.activation(out=Sst[:, n0:n0 + g, :].rearrange("p n t -> p (n t)"),
                                                                    in_=PS[bank][0:64, 0:g * 128], func=AF.Copy), psr(bank, 0, 512), ["E2"],
                       eng="act")
            if nseq == 1:
                pg.dma("sp", dst[2 * c:2 * c + 2, :, :].rearrange("h i j -> i h j"),
                       Sst[:, 0, :].rearrange("p (h j) -> p h j", h=2), reads=["E2"])
            else:
                for h in range(2):
                    pg.dma("sp", dst[:, 2 * c + h, :, :].rearrange("n i j -> i n j"),
                           Sst[:, :, h * 64:(h + 1) * 64], reads=["E2"])

        def state_in(c):
            for h in range(2):
                pg.dma("sp", Sld[:, :, h * 64:(h + 1) * 64],
                       s_wkv[:, 2 * c + h, :, :].rearrange("n i j -> i n j"), writes=["E1"])
            for n0 in range(0, NSQ, 4):
                bank = 6 + (n0 // 4) % 2
                for n in range(n0, n0 + 4):
                    sl = n - n0
                    pg.op("pe", lambda e, n=n, sl=sl, bank=bank: e.transpose(out=PS[bank][:, sl * 64:(sl + 1) * 64], in_=Sld[:, n, :],
                                                                             identity=ident[0:64, 0:64]),
                          reads=["E1", "cm"], writes=psr(bank, 0, 512))
                if (n0 // 4) % 2 == 0:
                    dv(lambda e, n0=n0, bank=bank: e.tensor_copy(out=Ss[:, n0:n0 + 4, :].rearrange("p n t -> p (n t)"), in_=PS[bank][:, 0:256]),
                       psr(bank, 0, 512), ["Ss"])
                else:
                    dv(lambda e, n0=n0, bank=bank: e.activation(out=Ss[:, n0:n0 + 4, :].rearrange("p n t -> p (n t)"), in_=PS[bank][:, 0:256],
                                                               func=AF.Copy), psr(bank, 0, 512), ["Ss"], eng="act")

        def mixer(blk):
            tiles = list(range(blk * NT, (blk + 1) * NT))
            has_sample = SAMPLE_TILE in tiles
            last_prompt = LAST_PROMPT_TILE in tiles
            npc = (NT - 1) * 128 if has_sample else N
            need_out = blk >= POST_BLK0
            need_carry = blk >= POST_BLK0 - 1
            for kind, cid, bank in (("wd", 8, 0), ("ad", 25, 1), ("gd0", 26, 2), ("gd1", 27, 3)):
                if kind.startswith("gd") and not need_carry:
                    continue
                off, w = CH_OFF[cid]
                proj("w_in", off, w, bank)
                lerp_chunk(cid, bank, pb[0], "pb0", t2, "t2", has_sample, last_prompt, npc)
                if kind == "wd":
                    dv(lambda e: e.activation(out=twd[:], in_=t2[0:96, :], func=AF.Tanh), ["t2"], ["twd"], eng="act")
                elif kind == "ad":
                    dv(lambda e: e.tensor_copy(out=qad[:], in_=t2[0:96, :]), ["t2"], ["qad"])
                elif need_out:
                    j = cid - 26
                    dv(lambda e, j=j: e.activation(out=sgd[:, j, :], in_=t2[:, :], func=AF.Sigmoid), ["t2"], ["sgd"], eng="act")
            for c in range(8):
                if cfg.get("mx_stage", 9) < 2:
                    break
                convert_slot(blk, c)
                for j, (kind, q, qres) in enumerate((("r", qr, "qr"), ("k", qk, "qk"), ("v", qv, "qv"))):
                    cid = chunk_id(kind, c)
                    off, w = CH_OFF[cid]
                    proj("w_in", off, w, j)
                    lerp_chunk(cid, j, pb[j], f"pb{j}", q, qres, has_sample, last_prompt, npc)
                pg.op("pe", lambda e, c=c: e.matmul(PS[3][:, 0:N], lhsT=lww[:, c * 128:(c + 1) * 128], rhs=twd[:], start=True, stop=True),
                      reads=["lww", "twd"], writes=psr(3, 0, N))
                pg.op("pe", lambda e, c=c: e.matmul(PS[4][:, 0:N], lhsT=lwa[:, c * 128:(c + 1) * 128], rhs=qad[:], start=True, stop=True),
                      reads=["lwa", "qad"], writes=psr(4, 0, N))
                for j in range(2):
                    if not need_out:
                        break
                    pg.op("pe", lambda e, c=c, j=j: e.matmul(PS[5][:, 0:N], lhsT=lwg[:, j, c * 128:(c + 1) * 128], rhs=sgd[:, j, :],
                                                              start=(j == 0), stop=(j == 1)),
                          reads=["lwg", "sgd"], writes=psr(5, 0, N))
                dv(lambda e, c=c: e.activation(out=lgw[:], in_=PS[3][:, 0:N], func=AF.Sigmoid, bias=pvt[:, c, 0:1]),
                   psr(3, 0, N) + ["pvt"], ["lgw"], eng="act")
                dv(lambda e: e.tensor_scalar(out=lgw[:], in0=lgw[:], scalar1=-0.6065306597126334, scalar2=None, op0=ALU.mult),
                   ["lgw"], ["lgw"], eng="pool")
                dv(lambda e, c=c: e.activation(out=av[:], in_=PS[4][:, 0:N], func=AF.Sigmoid, bias=pvt[:, c, 1:2]),
                   psr(4, 0, N) + ["pvt"], ["av"], eng="act")
                if need_out:
                    dv(lambda e: e.activation(out=gg[:], in_=PS[5][:, 0:N], func=AF.Copy), psr(5, 0, N), ["gg"], eng="act")
                dv(lambda e, c=c: e.tensor_scalar(out=kk[:], in0=qk[:], scalar1=pvt[:, c, 2:3], scalar2=None, op0=ALU.mult),
                   ["qk", "pvt"], ["kk"])
                dv(lambda e: e.tensor_tensor(out=t1[:], in0=kk[:], in1=kk[:], op=ALU.mult), ["kk"], ["t1"])
                pg.op("pe", lambda e: e.matmul(PS[6][:, 0:N], lhsT=BO, rhs=t1[:], start=True, stop=True),
                      reads=["cm", "t1"], writes=psr(6, 0, N))
                dv(lambda e: e.activation(out=t2[:], in_=PS[6][:, 0:N], func=AF.Sqrt), psr(6, 0, N), ["t2"], eng="act")
                dv(lambda e: e.tensor_scalar(out=t2[:], in0=t2[:], scalar1=1e-12, scalar2=None, op0=ALU.max), ["t2"], ["t2"])
                dv(lambda e: e.reciprocal(out=t2[:], in_=t2[:]), ["t2"], ["t2"])
                dv(lambda e: e.tensor_tensor(out=kk[:], in0=kk[:], in1=t2[:], op=ALU.mult), ["kk", "t2"], ["kk"])
                dv(lambda e, c=c: e.tensor_scalar(out=kp[:], in0=av[:], scalar1=pvt[:, c, 3:4], scalar2=pvt[:, c, 7:8],
                                                  op0=ALU.mult, op1=ALU.add), ["av", "pvt"], ["kp"])
                dv(lambda e: e.tensor_tensor(out=kp[:], in0=kp[:], in1=qk[:], op=ALU.mult), ["kp", "qk"], ["kp"])
                dv(lambda e: e.tensor_tensor(out=ka[:], in0=kk[:], in1=av[:], op=ALU.mult), ["kk", "av"], ["ka"], eng="pool")
                if need_out:
                    dv(lambda e, c=c: e.scalar_tensor_tensor(out=t1[:], in0=qr[:], scalar=pvt[:, c, 4:5], in1=kp[:],
                                                             op0=ALU.mult, op1=ALU.mult), ["qr", "kp", "pvt"], ["t1"])
                    pg.op("pe", lambda e: e.matmul(PS[7][:, 0:N], lhsT=BO, rhs=t1[:], start=True, stop=True),
                          reads=["cm", "t1"], writes=psr(7, 0, N))
                    dv(lambda e: e.tensor_tensor(out=bon[:], in0=qv[:], in1=PS[7][:, 0:N], op=ALU.mult),
                       ["qv"] + psr(7, 0, N), ["bon"])
                if cfg.get("mx_stage", 9) >= 3:
                    ntp = NT - 1 if has_sample else NT
                    last_ti = tiles.index(LAST_PROMPT_TILE) if last_prompt else -1
                    scan_block(c, ntp, last_ti, need_out)
                    if has_sample:
                        pg.barrier()
                        if last_prompt:
                            state_out(c, 1, wkv_p)
                        state_in(c)
                        scan_tile(c, npc, NSQ, False)
                        state_out(c, NSQ, wkv_s)
                        pg.barrier()
            for cc in range(8):
                if cfg.get("mx_stage", 9) < 4:
                    break
                if not need_carry:
                    break
                if need_out:
                    proj("w_in", OFF_CB + cc * 128, 128, 0)
                proj("w_in", OFF_CC + cc * 128, 128, 1)
                proj("w_in", OFF_CX + cc * 128, 128, 2)
                dv(lambda e, cc=cc: e.tensor_copy(out=ub[:, 0:2], in_=ucar[:, cc, :]), ["ucar"], ["ub"])
                dv(lambda e: e.activation(out=t1[:], in_=PS[1][:, 0:N], func=AF.Copy), psr(1, 0, N), ["t1"], eng="act")
                dv(lambda e: e.tensor_tensor(out=ub[:, 2:N + 2], in0=t1[:], in1=PS[2][:, 0:N], op=ALU.mult),
                   ["t1"] + psr(2, 0, N), ["ub"])
                if npc > 0:
                    dv(lambda e, cc=cc: e.tensor_copy(out=ucar[:, cc, :], in_=ub[:, npc:npc + 2]), ["ub"], ["ucar"])
                if not need_out:
                    continue
                dv(lambda e, cc=cc: e.tensor_scalar(out=cvt[:], in0=ub[:, 0:N], scalar1=cwt[:, cc, 0:1], scalar2=None, op0=ALU.mult),
                   ["ub", "cwt"], ["t2"])
                dv(lambda e, cc=cc: e.scalar_tensor_tensor(out=cvt[:], in0=ub[:, 1:N + 1], scalar=cwt[:, cc, 1:2], in1=cvt[:],
                                                           op0=ALU.mult, op1=ALU.add), ["ub", "cwt", "t2"], ["t2"])
                dv(lambda e, cc=cc: e.scalar_tensor_tensor(out=cvt[:], in0=ub[:, 2:N + 2], scalar=cwt[:, cc, 2:3], in1=cvt[:],
                                                           op0=ALU.mult, op1=ALU.add), ["ub", "cwt", "t2"], ["t2"])
                if last_prompt:
                    pg.dma("sp", conv_p[:, cc * 128:(cc + 1) * 128].rearrange("t c -> c t"), ub[:, npc:npc + 2], reads=["ub"])
                if has_sample:
                    k = cc % 2
                    pg.dma("sp", stg[k][:, :], s_conv.rearrange("n t c -> (n t) c")[:, cc * 128:(cc + 1) * 128], writes=[f"stg{k}"])
                    pg.op("pe", lambda e, k=k: e.transpose(out=PS[7][:, k * 128:k * 128 + 32], in_=stg[k][:, :], identity=ident[0:32, 0:32]),
                          reads=[f"stg{k}", "cm"], writes=["ps7"])
                    dv(lambda e, k=k: e.tensor_copy(out=uS[:, :, 0:2], in_=PS[7][:, k * 128:k * 128 + 32].rearrange("p (n t) -> p n t", t=2)),
                       ["ps7"], ["uS"])
                    dv(lambda e: e.tensor_copy(out=uS[:, :, 2:LSQ + 2], in_=ub[:, npc + 2:npc + 130].rearrange("p (n t) -> p n t", t=LSQ)),
                       ["ub"], ["uS"])
                    cS = cvt[:, npc:npc + 128].rearrange("p (n t) -> p n t", t=LSQ)
                    dv(lambda e, cc=cc: e.tensor_scalar(out=cS, in0=uS[:, :, 0:LSQ], scalar1=cwt[:, cc, 0:1], scalar2=None, op0=ALU.mult),
                       ["uS", "cwt", "t2"], ["t2"])
                    dv(lambda e, cc=cc: e.scalar_tensor_tensor(out=cS, in0=uS[:, :, 1:LSQ + 1], scalar=cwt[:, cc, 1:2], in1=cS,
                                                               op0=ALU.mult, op1=ALU.add), ["uS", "cwt", "t2"], ["t2"])
                    dv(lambda e, cc=cc: e.scalar_tensor_tensor(out=cS, in0=uS[:, :, 2:LSQ + 2], scalar=cwt[:, cc, 2:3], in1=cS,
                                                               op0=ALU.mult, op1=ALU.add), ["uS", "cwt", "t2"], ["t2"])
                    k = 2 + cc % 2
                    dv(lambda e: e.tensor_copy(out=ctmp.rearrange("p (n t) -> p n t", t=2), in_=uS[:, :, LSQ:LSQ + 2]), ["uS"], ["ctmp"])
                    pg.op("pe", lambda e, k=k: e.transpose(out=PS[6][0:32, k * 128:(k + 1) * 128], in_=ctmp, identity=ident),
                          reads=["ctmp", "cm"], writes=["ps6"])
                    dv(lambda e, k=k: e.tensor_copy(out=stg[k][:, :], in_=PS[6][0:32, k * 128:(k + 1) * 128]), ["ps6"], [f"stg{k}"])
                    pg.dma("sp", conv_s.rearrange("n t c -> (n t) c")[:, cc * 128:(cc + 1) * 128], stg[k][:, :], reads=[f"stg{k}"])
                dv(lambda e, cc=cc: e.tensor_tensor(out=act[:, 8 + cc, :], in0=cvt[:], in1=PS[0][:, 0:N], op=ALU.mult),
                   ["t2"] + psr(0, 0, N), [f"act{8 + cc}"])
            if last_prompt:
                for cid, (off, w) in CH_OFF.items():
                    pg.dma("sp", shift_p[off:off + w].rearrange("(c o) -> c o", o=1), shP[0:w, cid:cid + 1], reads=["shP"])
            if has_sample:
                for cid, (off, w) in CH_OFF.items():
                    k = cid % 4
                    pg.op("pe", lambda e, k=k, w=w, cid=cid: e.transpose(out=PS[7][0:NSQ, k * 128:k * 128 + w], in_=shO[0:w, cid, :],
                                                                         identity=ident[0:w, 0:w]), reads=["shO", "cm"], writes=["ps7"])
                    dv(lambda e, k=k, w=w: e.tensor_copy(out=stg[k][0:NSQ, 0:w], in_=PS[7][0:NSQ, k * 128:k * 128 + w]), ["ps7"], [f"stg{k}"])
                    pg.dma("sp", shift_s[:, off:off + w], stg[k][0:NSQ, 0:w], reads=[f"stg{k}"])

        def emit():
            wnext["w"] = 0
            wnext["d"] = 0
            wdone.clear()
            conv_pos.clear()
            pg.dma("sp", cm[:], cmat, writes=["cm"])
            pg.dma("sp", cmask[:].rearrange("p n t -> p (n t)"), cmask_d, writes=["cmask"])
            pg.dma("sp", rmask[:], rmask_d, writes=["rmask"])
            pg.dma("sp", gv[:], gvec, writes=["gv"])
            pg.dma("sp", gf[:], gfin, writes=["gf"])
            pg.dma("sp", mu[:], mu_fm, writes=["mu"])
            pg.dma("sp", pvt[:], pv, writes=["pvt"])
            pg.dma("sp", cwt[:], cw, writes=["cwt"])
            pg.dma("pool", lww[:], lw_w, writes=["lww"])
            pg.dma("pool", lwa[:], lw_a, writes=["lwa"])
            pg.dma("pool", lwg[:], lw_g.rearrange("(j p) c -> p j c", p=128), writes=["lwg"])
            for i, (cid, (off, w)) in enumerate(CH_OFF.items()):
                k = i % 4
                pg.dma("sp", stg[k][0:NSQ, 0:w], s_shift[:, off:off + w], writes=[f"stg{k}"])
                pg.op("pe", lambda e, k=k, w=w: e.transpose(out=PS[7][0:w, k * 128:k * 128 + NSQ], in_=stg[k][0:NSQ, 0:w],
                                                            identity=ident[0:NSQ, 0:NSQ]), reads=[f"stg{k}", "cm"], writes=["ps7"])
                dv(lambda e, k=k, w=w, cid=cid: e.tensor_copy(out=shT[0:w, cid, :], in_=PS[7][0:w, k * 128:k * 128 + NSQ]),
                   ["ps7"], ["shT"])
            pg.barrier()
            dv(lambda e: e.tensor_scalar(out=omu[:], in0=mu[:], scalar1=-1.0, scalar2=1.0, op0=ALU.mult, op1=ALU.add),
               ["mu"], ["omu"])
            dv(lambda e: e.memset(carry[:], 0.0), [], ["carry"], eng="pool")
            dv(lambda e: e.tensor_copy(out=identb[:], in_=ident), ["cm"], ["identb"])
            dv(lambda e: e.tensor_tensor(out=I2[:], in0=cm[:, 0:64], in1=cm[:, 64:128], op=ALU.add), ["cm"], ["I2"])
            dv(lambda e: e.memset(scanP[:], 1.0), [], ["scanP"], eng="pool")
            dv(lambda e: e.memset(scanP[:].rearrange("p (n t) -> p n t", t=128)[:, :, 0:1], 0.0), [], ["scanP"], eng="pool")
            dv(lambda e: e.memset(Sb[:].rearrange("p a b -> p (a b)"), 0.0), [], [f"Sb{c}" for c in range(8)], eng="pool")
            dv(lambda e: e.memset(ucar[:].rearrange("p a b -> p (a b)"), 0.0), [], ["ucar"], eng="pool")
            dv(lambda e: e.memset(Sp[:].rearrange("p a b -> p (a b)"), 0.0), [], [f"Sp{c}" for c in range(8)], eng="pool")
            dv(lambda e: e.memset(stat[:], 0.0), [], ["stat0", "stat1", "stat2"], eng="pool")
            for blk in range(NBLK):
                post = blk >= POST_BLK0
                for tt in range(NT):
                    tile = blk * NT + tt
                    pg.dma("sp", xb[tt][:], xw[tile * 128:(tile + 1) * 128, :], writes=[f"xb{tt}"])
                if blk >= cfg.get("nblk", NBLK):
                    continue
                norm_T(0)
                if cfg.get("ffn", True):
                    ffn("w1g", "w1u", "w1d")
                norm_T(KC)
                if cfg.get("mixer", True):
                    mixer(blk)
                if post and cfg.get("post", True):
                    down_proj(KC, "w_out", 0, 1.0, list(range(NT)))
                    norm_T(2 * KC)
                    ffn("w2g", "w2u", "w2d")
                    for tt in range(NT):
                        xr = f"xb{tt}"
                        pg.op("act", lambda e, tt=tt: e.activation(out=junk[:], in_=xb[tt][:], func=AF.Square, accum_out=stat[:, 0:1]),
                              reads=[xr], writes=JUNK + ["stat0"])
                        pg.op("dve", lambda e: e.tensor_scalar(out=stat[:, 1:2], in0=stat[:, 0:1], scalar1=1.0 / D, scalar2=RMS_EPS,
                                                               op0=ALU.mult, op1=ALU.add), reads=["stat0"], writes=["stat1"])
                        pg.op("act", lambda e: e.activation(out=stat[:, 1:2], in_=stat[:, 1:2], func=AF.Sqrt), reads=["stat1"], writes=["stat1"])
                        pg.op("dve", lambda e: e.reciprocal(out=stat[:, 2:3], in_=stat[:, 1:2]), reads=["stat1"], writes=["stat2"])
                        pg.op("dve", lambda e, tt=tt: e.scalar_tensor_tensor(out=xn[:], in0=xb[tt][:], scalar=stat[:, 2:3], in1=gf[:],
                                                                             op0=ALU.mult, op1=ALU.mult),
                              reads=[xr, "stat2", "gf"], writes=["xn"])
                        ot = (blk - POST_BLK0) * NT + tt
                        pg.dma("sp", y_out[ot * 128:(ot + 1) * 128, :], xn[:], reads=["xn"])
            pg.finish()

        pg.dry = True
        emit()
        pg.dry = False
        emit()
    return nc


_CACHE = {}
_DEBUG = {}


def _consts():
    cmx = np.zeros((128, 2048), np.float32)
    cmx[:, 0:128] = np.eye(128)
    bo = np.kron(np.eye(2), np.ones((64, 64))).astype(np.float32)
    cmx[:, 128:256] = bo
    cmx[:, 256:384] = bo / 64.0
    s = np.arange(128)[:, None]
    t = np.arange(128)[None, :]
    for nseq, a0, l0, sc0 in ((1, 384, 640, 1152), (NSQ, 768, 1024, 1280)):
        L = 128 // nseq
        same = (s // L) == (t // L)
        MsT = ((s < t) & same).astype(np.float32)
        MiT = ((s <= t) & same).astype(np.float32)
        cmx[:, a0:a0 + 128] = MsT
        cmx[:, a0 + 128:a0 + 256] = MiT
        cmx[:, l0:l0 + 128] = MsT.T
        sm = np.ones((128, 128), np.float32)
        sm[:, (np.arange(128) % L) == 0] = 0.0
        cmx[:, sc0:sc0 + 128] = sm
    cmask = np.zeros((128, NSQ, 128), np.float32)
    for n in range(NSQ):
        cmask[:, n, n * LSQ:(n + 1) * LSQ] = 1.0
    rmask = np.zeros((128, NSQ), np.float32)
    for n in range(NSQ):
        rmask[n * LSQ:(n + 1) * LSQ, n] = 1.0
    return cmx, cmask.reshape(128, NSQ * 128), rmask


def kernel(x_prompt, x_sample, state_wkv, state_shift, state_conv, meta_tokens,
           g_ffn1, ffn1_gate, ffn1_up, ffn1_down, g_mix, w_in, mu_shift, w0, w_lora_w,
           a0, w_lora_a, w_lora_g, k_k, k_a, r_k, ln_x_w, ln_x_b, conv_w, w_out,
           g_ffn2, ffn2_gate, ffn2_up, ffn2_down, g_final):
    f = lambda a: np.ascontiguousarray(np.asarray(a, dtype=np.float32))
    x_prompt, x_sample = f(x_prompt), f(x_sample)
    if "nc" not in _CACHE:
        _CACHE["nc"] = build_program(_DEBUG.get("cfg"))
    nc = _CACHE["nc"]
    cmx, cmask, rmask = _consts()
    fm = lambda v: np.ascontiguousarray(f(v).reshape(-1, 128).T)
    gvec = np.concatenate([fm(g_ffn1[0]), fm(g_mix[0]), fm(g_ffn2[0])], axis=1)
    gfin = np.ascontiguousarray(np.broadcast_to(f(g_final)[None, :], (128, D)))
    mu_fm = np.zeros((128, 28), np.float32)
    mus = f(mu_shift[0])
    offs = {}
    for c in range(8):
        offs[c] = (OFF_R + c * 128, 128); offs[9 + c] = (OFF_K + c * 128, 128); offs[17 + c] = (OFF_V + c * 128, 128)
    offs[8] = (OFF_WD, 96); offs[25] = (OFF_AD, 96); offs[26] = (OFF_GD, 128); offs[27] = (OFF_GD + 128, 128)
    for cid, (o, w) in offs.items():
        mu_fm[0:w, cid] = mus[o:o + w]
    pvv = np.zeros((128, 8, 8), np.float32)
    for i, v in enumerate((w0[0], a0[0], k_k[0], k_a[0], f(r_k[0]).reshape(-1), ln_x_w[0], ln_x_b[0])):
        pvv[:, :, i] = f(v).reshape(8, 128).T
    pvv[:, :, 7] = 1.0 - pvv[:, :, 3]
    cwv = np.ascontiguousarray(f(conv_w[0]).reshape(3, 8, 128).transpose(2, 1, 0))
    shared = {
        "w1g": f(ffn1_gate[0]), "w1u": f(ffn1_up[0]), "w1d": f(ffn1_down[0]),
        "w2g": f(ffn2_gate[0]), "w2u": f(ffn2_up[0]), "w2d": f(ffn2_down[0]),
        "w_in": f(w_in[0]), "w_out": f(w_out[0]),
        "lw_w": f(w_lora_w[0]), "lw_a": f(w_lora_a[0]), "lw_g": f(w_lora_g[0]),
        "gvec": gvec, "gfin": gfin, "mu_fm": mu_fm, "pv": pvv, "cw": cwv,
        "cmat": cmx, "cmask": cmask, "rmask": rmask,
    }
    meta = f(meta_tokens)
    in_maps = []
    for core in range(8):
        b, half = core // 2, core % 2
        xw = np.zeros((NTILES * 128, D), np.float32)
        if half == 0:
            xw[8 * 128 + 112:9 * 128] = meta
            xw[9 * 128:17 * 128] = x_prompt[b, 0:1024]
        else:
            xw[112:128] = meta
            xw[128:17 * 128] = x_prompt[b]
        xw[17 * 128:] = x_sample[core * NSQ:(core + 1) * NSQ].reshape(128, D)
        m = dict(shared)
        m["xw"] = xw
        m["s_wkv"] = f(state_wkv[0, core * NSQ:(core + 1) * NSQ])
        m["s_shift"] = f(state_shift[0, core * NSQ:(core + 1) * NSQ])
        m["s_conv"] = f(state_conv[0, core * NSQ:(core + 1) * NSQ])
        in_maps.append(m)
    if "cores" in _DEBUG:
        sel = _DEBUG["cores"]
        res = run_bass_kernel_spmd(nc, [in_maps[i] for i in sel], core_ids=list(range(len(sel))))
        R = [res.results[sel.index(i)] if i in sel else None for i in range(8)]
    else:
        res = run_bass_kernel_spmd(nc, in_maps, core_ids=list(range(8)))
        R = res.results
    B = x_prompt.shape[0]
    y_prompt = np.zeros((B, 2048, D), np.float32)
    y_sample = np.zeros((128, 8, D), np.float32)
    wkv_p = np.zeros((1, B, NH, HD, HD), np.float32)
    shift_p = np.zeros((1, B, RWKV_PROJ), np.float32)
    conv_p = np.zeros((1, B, 2, GR), np.float32)
    wkv_s = np.zeros((1, 128, NH, HD, HD), np.float32)
    shift_s = np.zeros((1, 128, RWKV_PROJ), np.float32)
    conv_s = np.zeros((1, 128, 2, GR), np.float32)
    for core in range(8):
        b, half = core // 2, core % 2
        r = R[core]
        if r is None:
            continue
        y_prompt[b, half * 1024:(half + 1) * 1024] = r["y_out"][0:1024]
        y_sample[core * NSQ:(core + 1) * NSQ] = r["y_out"][1024:1152].reshape(NSQ, 8, D)
        if half == 1:
            wkv_p[0, b] = r["wkv_p"]
            shift_p[0, b] = r["shift_p"]
            conv_p[0, b] = r["conv_p"]
        wkv_s[0, core * NSQ:(core + 1) * NSQ] = r["wkv_s"]
        shift_s[0, core * NSQ:(core + 1) * NSQ] = r["shift_s"]
        conv_s[0, core * NSQ:(core + 1) * NSQ] = r["conv_s"]
    return (y_prompt, y_sample, wkv_p, shift_p, conv_p, wkv_s, shift_s, conv_s)
```
